# Optimizing a Trainium2 kernel written in Bass

```python
import math
import jax, jax.numpy as jnp
from jax import lax
import numpy as np

D_MODEL = 2048
BATCH = 4
SEQ = 2048
DEPTH = 1

CHUNK = 64
Q_BLOCK = 128
D_MIX = D_MODEL
DIFF_HEADS = 8
DIFF_WIDTH = D_MIX // 2
DIFF_HEAD_DIM = DIFF_WIDTH // DIFF_HEADS
DIFF_QK_DIM = DIFF_HEAD_DIM // 2
GLA_HEADS = 4
GLA_WIDTH = D_MIX - DIFF_WIDTH
GLA_V_DIM = GLA_WIDTH // GLA_HEADS
GLA_K_DIM = GLA_V_DIM // 2
GLA_GATE_RANK = 16
GLA_GATE_TAU = 16.0
ROPE_THETA = 10000.0
NORM_EPS = 1e-6

IN_SPLITS = [
    DIFF_WIDTH,
    DIFF_WIDTH,
    DIFF_WIDTH,
    DIFF_WIDTH,
    GLA_HEADS * GLA_K_DIM,
    GLA_HEADS * GLA_K_DIM,
    GLA_WIDTH,
    GLA_WIDTH,
    GLA_GATE_RANK,
]
D_IN_PROJ = int(sum(IN_SPLITS))
IN_OFFSETS = [int(o) for o in np.cumsum(IN_SPLITS)[:-1]]

kernel_name = "hymba_diffattn_gla_chunk_causal"


def lambda_init_fn(layer_idx):
    return 0.8 - 0.6 * math.exp(-0.3 * layer_idx)


def rmsnorm(x, w):
    xf = x.astype(jnp.float32)
    y = xf * lax.rsqrt(jnp.mean(xf * xf, axis=-1, keepdims=True) + NORM_EPS)
    return y * w.astype(jnp.float32)


def rope(x, pos):
    dh = x.shape[-1]
    inv_freq = ROPE_THETA ** (-jnp.arange(0, dh, 2, dtype=jnp.float32) / dh)
    ang = pos[:, None] * inv_freq[None, :]
    bshape = (1, ang.shape[0]) + (1,) * (x.ndim - 3) + (dh // 2,)
    cos = jnp.cos(ang).reshape(bshape)
    sin = jnp.sin(ang).reshape(bshape)
    x1, x2 = x[..., : dh // 2], x[..., dh // 2:]
    return jnp.concatenate([x1 * cos - x2 * sin, x2 * cos + x1 * sin], axis=-1)


def diff_attention(q, k, v, lam):
    S = q.shape[1]
    scale = DIFF_QK_DIM ** -0.5
    chunk_id = jnp.arange(S) // CHUNK
    outs = []
    for blk in range(S // Q_BLOCK):
        s0, s1 = blk * Q_BLOCK, (blk + 1) * Q_BLOCK
        qb, kb, vb = q[:, s0:s1], k[:, :s1], v[:, :s1]
        s = jnp.einsum('bqhmd,bkhmd->bmhqk', qb, kb) * scale
        mask = chunk_id[None, :s1] <= chunk_id[s0:s1, None]
        s = jnp.where(mask, s, -jnp.inf)
        p = jax.nn.softmax(s, axis=-1)
        a = p[:, 0] - lam * p[:, 1]
        outs.append(jnp.einsum('bhqk,bkhd->bqhd', a, vb))
    return jnp.concatenate(outs, axis=1)


def gla_chunk_causal(q, k, v, log_a):
    B, S, H, dk = q.shape
    dv = v.shape[-1]
    nc = S // CHUNK
    def to_chunks(t):
        return t.reshape(B, nc, CHUNK, H, t.shape[-1]).transpose(1, 0, 3, 2, 4)
    qc, kc, vc, lc = to_chunks(q), to_chunks(k), to_chunks(v), to_chunks(log_a)
    b = jnp.cumsum(lc, axis=3)
    b_tot = b[:, :, :, -1:, :]
    k_dec = kc * jnp.exp(b_tot - b)
    chunk_decay = jnp.exp(b_tot[:, :, :, 0, :])

    def step(state, inp):
        qi, ki, vi, di = inp
        state = di[..., None] * state + jnp.einsum('bhck,bhcv->bhkv', ki, vi)
        return state, jnp.einsum('bhck,bhkv->bhcv', qi, state)

    state0 = jnp.zeros((B, H, dk, dv), jnp.float32)
    _, o = lax.scan(step, state0, (qc, k_dec, vc, chunk_decay))
    return o.transpose(1, 0, 3, 2, 4).reshape(B, S, H, dv)


def setup_inputs(seed: int = 0) -> dict:
    key = jax.random.key(seed)
    ks = jax.random.split(key, 17)
    f32 = jnp.float32
    nrm = lambda k, shp, s: jax.random.normal(k, shp, f32) * s
    return {
        "x": nrm(ks[0], (BATCH, SEQ, D_MODEL), 1.0),
        "c": nrm(ks[1], (BATCH, D_MODEL), 1.0),
        "norm_w": 1.0 + nrm(ks[2], (DEPTH, D_MODEL), 0.02),
        "w_ada": nrm(ks[3], (DEPTH, D_MODEL, 3 * D_MODEL), 0.5 * D_MODEL ** -0.5),
        "b_ada": nrm(ks[4], (DEPTH, 3 * D_MODEL), 0.02),
        "w_in": nrm(ks[5], (DEPTH, D_MODEL, D_IN_PROJ), D_MODEL ** -0.5),
        "lambda_q1": nrm(ks[6], (DEPTH, DIFF_QK_DIM), 0.1),
        "lambda_k1": nrm(ks[7], (DEPTH, DIFF_QK_DIM), 0.1),
        "lambda_q2": nrm(ks[8], (DEPTH, DIFF_QK_DIM), 0.1),
        "lambda_k2": nrm(ks[9], (DEPTH, DIFF_QK_DIM), 0.1),
        "diff_norm_w": 1.0 + nrm(ks[10], (DEPTH, DIFF_HEAD_DIM), 0.02),
        "gla_gate_w2": nrm(ks[11], (DEPTH, GLA_GATE_RANK, GLA_HEADS * GLA_K_DIM), GLA_GATE_RANK ** -0.5),
        "gla_gate_b": nrm(ks[12], (DEPTH, GLA_HEADS * GLA_K_DIM), 0.1),
        "gla_norm_w": 1.0 + nrm(ks[13], (DEPTH, GLA_V_DIM), 0.02),
        "w_out": nrm(ks[14], (DEPTH, D_MIX, D_MODEL), D_MIX ** -0.5),
        "final_norm_w": 1.0 + nrm(ks[15], (D_MODEL,), 0.02),
    }


def reference(x, c, norm_w, w_ada, b_ada, w_in, lambda_q1, lambda_k1, lambda_q2, lambda_k2,
              diff_norm_w, gla_gate_w2, gla_gate_b, gla_norm_w, w_out, final_norm_w):
    B, S, _ = x.shape
    f32 = jnp.float32
    pos = jnp.arange(S, dtype=f32)
    h_res = x.astype(f32)
    c_act = jax.nn.silu(c.astype(f32))
    for l in range(DEPTH):
        lam_init = lambda_init_fn(l)
        mod = c_act @ w_ada[l].astype(f32) + b_ada[l].astype(f32)
        shift, scale, gate = jnp.split(mod, 3, axis=-1)
        hn = rmsnorm(h_res, norm_w[l]) * (1.0 + scale[:, None, :]) + shift[:, None, :]

        proj = hn @ w_in[l].astype(f32)
        dq, dk, dv, dg, gq, gk, gv, gg, glr = jnp.split(proj, IN_OFFSETS, axis=-1)

        dq = rope(dq.reshape(B, S, DIFF_HEADS, 2, DIFF_QK_DIM), pos)
        dk = rope(dk.reshape(B, S, DIFF_HEADS, 2, DIFF_QK_DIM), pos)
        dv = dv.reshape(B, S, DIFF_HEADS, DIFF_HEAD_DIM)
        lam = (jnp.exp(jnp.sum(lambda_q1[l].astype(f32) * lambda_k1[l].astype(f32)))
               - jnp.exp(jnp.sum(lambda_q2[l].astype(f32) * lambda_k2[l].astype(f32)))
               + lam_init)
        a_out = diff_attention(dq, dk, dv, lam)
        a_out = rmsnorm(a_out, diff_norm_w[l]) * (1.0 - lam_init)
        a_out = a_out.reshape(B, S, DIFF_WIDTH) * jax.nn.silu(dg)

        gq = gq.reshape(B, S, GLA_HEADS, GLA_K_DIM) * (GLA_K_DIM ** -0.5)
        gk = gk.reshape(B, S, GLA_HEADS, GLA_K_DIM)
        gv = gv.reshape(B, S, GLA_HEADS, GLA_V_DIM)
        log_a = jax.nn.log_sigmoid(glr @ gla_gate_w2[l].astype(f32) + gla_gate_b[l].astype(f32)) / GLA_GATE_TAU
        log_a = log_a.reshape(B, S, GLA_HEADS, GLA_K_DIM)
        b_out = gla_chunk_causal(gq, gk, gv, log_a)
        b_out = rmsnorm(b_out, gla_norm_w[l]).reshape(B, S, GLA_WIDTH) * jax.nn.silu(gg)

        mixed = jnp.concatenate([a_out, b_out], axis=-1) @ w_out[l].astype(f32)
        h_res = h_res + gate[:, None, :] * mixed
    return rmsnorm(h_res, final_norm_w).astype(x.dtype)
```

```python
import os
from contextlib import ExitStack

import numpy as np
import concourse.bass as bass
import concourse.mybir as mybir
from concourse.bass_utils import run_bass_kernel_spmd

F32 = mybir.dt.float32
BF16 = mybir.dt.bfloat16
AF = mybir.ActivationFunctionType
ALU = mybir.AluOpType
ds = bass.ds

T = 2048
D = 2048
NT = 16
EPS = 1e-6
ENGS = ["pe", "act", "dve", "pool", "sp"]
NSLOT = {"sp": 24, "pool": 12}
SEMKEYS = (["pe", "act", "dve", "pool", "cc0", "cc1", "cc2", "cc3"]
           + [f"dq_{q}{i}" for q, n in NSLOT.items() for i in range(n)])
DEBUG = bool(int(os.environ.get("KDEBUG", "0")))
STOP = int(os.environ.get("KSTOP", "99"))
KEV = os.environ.get("KEV", "")
FILL_BURST = int(os.environ.get("KFB", "2"))
FILL_EVERY = int(os.environ.get("KFE", "2"))


class _Stop(Exception):
    pass


class Prog:
    def __init__(self, sems):
        self.sem = sems
        self.q = {e: [] for e in ENGS}
        self.cnt = {k: 0 for k in SEMKEYS}
        self.waited = {e: {k: 0 for k in SEMKEYS} for e in ENGS}
        self.res = {}
        self.dnext = {q: 0 for q in NSLOT}
        self.ncc = 0

    def _deps(self, reads, writes):
        deps = {}

        def add(k, v):
            if deps.get(k, 0) < v:
                deps[k] = v

        for r in reads:
            st = self.res.get(r)
            if st and st["w"]:
                add(*st["w"])
        for w in writes:
            st = self.res.get(w)
            if st:
                if st["w"]:
                    add(*st["w"])
                for k, v in st["r"].items():
                    add(k, v)
        return deps

    def _commit(self, reads, writes, ev):
        for r in reads:
            st = self.res.setdefault(r, {"w": None, "r": {}})
            if st["r"].get(ev[0], 0) < ev[1]:
                st["r"][ev[0]] = ev[1]
        for w in writes:
            self.res[w] = {"w": ev, "r": {}}

    def _waits(self, eng, deps):
        for k, v in deps.items():
            if k == "pe":
                assert v <= self.cnt["pe"], "unresolved PE milestone"
            if self.waited[eng][k] < v:
                self.waited[eng][k] = v
                sem = self.sem[k]
                self.q[eng].append(lambda e, sem=sem, v=v: e.wait_ge(sem, v))

    def op(self, eng, fn, reads=(), writes=(), milestone=True):
        deps = self._deps(reads, writes)
        if eng == "pe":
            deps.pop("pe", None)
        self._waits(eng, deps)
        if eng == "pe" and not milestone:
            ev = ("pe", self.cnt["pe"] + 1)
            self.q[eng].append(fn)
        else:
            self.cnt[eng] += 1
            ev = (eng, self.cnt[eng])
            self.waited[eng][eng] = max(self.waited[eng][eng], 0)
            sem = self.sem[eng]
            self.q[eng].append(lambda e, fn=fn, sem=sem: fn(e).then_inc(sem, 1))
        self._commit(reads, writes, ev)

    def dma(self, qeng, fn, reads=(), writes=()):
        deps = self._deps(reads, writes)
        k = f"dq_{qeng}{self.dnext[qeng] % NSLOT[qeng]}"
        self.dnext[qeng] += 1
        if self.cnt[k] > 0 and deps.get(k, 0) < self.cnt[k]:
            deps[k] = self.cnt[k]
        self._waits(qeng, deps)
        self.cnt[k] += 16
        ev = (k, self.cnt[k])
        sem = self.sem[k]
        self.q[qeng].append(lambda e, fn=fn, sem=sem: fn(e).then_inc(sem, 16))
        self._commit(reads, writes, ev)

    def cc(self, fn, reads=(), writes=()):
        deps = self._deps(reads, writes)
        self._waits("pool", deps)
        k = f"cc{self.ncc}"
        self.ncc += 1
        self.cnt[k] += 1
        ev = (k, self.cnt[k])
        sem = self.sem[k]
        self.q["pool"].append(lambda e, fn=fn, sem=sem: fn(e).then_inc(sem, 1))
        self._commit(reads, writes, ev)

    def barrier(self, keep=()):
        kept = {n: self.res[n] for n in keep if n in self.res}
        skip = {st["w"][0] for st in kept.values() if st["w"] and st["w"][0].startswith("dq_")}
        tgt = {k: v for k, v in self.cnt.items() if k not in skip}
        for eng in ENGS:
            self._waits(eng, dict(tgt))
        self.res = kept


def build_program():
    nc = bass.Bass("TRN2", target_bir_lowering=False)

    def din(name, shape, dt=F32):
        return nc.dram_tensor(name, shape, dt, kind="ExternalInput")

    x_d = din("x", [T, D])
    xo_d = din("xo", [1024, D])
    cT_d = din("cT", [128, 16])
    wada_d = din("wada", [128, 16 * 3072])
    bada_d = din("bada", [1, 3072])
    nwT_d = din("nwT", [128, 16])
    win_d = din("win", [7, 128, 16 * 512])
    wglr_d = din("wglr", [128, 16 * 16])
    w2aug_d = din("w2aug", [17, 256])
    lam4_d = din("lam4", [4, 64])
    dnw_d = din("dnw", [1, 128])
    gnw_d = din("gnw", [1, 256])
    fnw_d = din("fnw", [1, D])
    wout_d = din("wout", [128, 16 * D])
    ident_d = din("ident", [128, 128])
    pm_d = din("pm", [128, 128])
    U_d = din("U", [128, 128])
    ind_d = din("ind", [128, 2])
    cos_d = din("cosT", [128, T])
    sin_d = din("sinT", [128, T])
    out_d = nc.dram_tensor("out", [1024, D], F32, kind="ExternalOutput")
    mss_in = nc.dram_tensor("mss_in", [1, 2048], F32)
    mss_out = nc.dram_tensor("mss_out", [2, 2048], F32)
    mg_in = nc.dram_tensor("mg_in", [1, 1024], F32)
    mg_out = nc.dram_tensor("mg_out", [2, 1024], F32)
    ya_in = nc.dram_tensor("ya_in", [T, 512], BF16)
    ya_out = nc.dram_tensor("ya_out", [2 * T, 512], BF16)
    yg_in = nc.dram_tensor("yg_in", [T, 512], BF16)
    yg_out = nc.dram_tensor("yg_out", [2 * T, 512], BF16)
    if DEBUG:
        dbg_hn = nc.dram_tensor("dbg_hn", [128, 16 * T], BF16, kind="ExternalOutput")
        dbg_y = nc.dram_tensor("dbg_y", [T, 1024], BF16, kind="ExternalOutput")
        dbg_q = nc.dram_tensor("dbg_q", [128, 2 * T], BF16, kind="ExternalOutput")
        dbg_mod = nc.dram_tensor("dbg_mod", [1, 3 * D], F32, kind="ExternalOutput")
        dbg_sm = nc.dram_tensor("dbg_sm", [128, 96], F32, kind="ExternalOutput")
        dbg_xh = nc.dram_tensor("dbg_xh", [128, D], BF16, kind="ExternalOutput")

    def bcast_rows(dh, n):
        return bass.AP(dh, 0, [[0, 128], [1, n]])

    with ExitStack() as top:
        sems = {k: top.enter_context(nc.semaphore("s_" + k)) for k in SEMKEYS}
        P = Prog(sems)
        PID = {}

        def sb(stack, name, shape, dt):
            return stack.enter_context(nc.sbuf_tensor("sb_" + name, shape, dt))

        ps = [top.enter_context(nc.psum_tensor(f"ps{i}", [128, 512], F32)) for i in range(7)]
        psT = top.enter_context(nc.psum_tensor("psT", [128, 8, 128], BF16))

        ident = sb(top, "ident", [128, 128], BF16)
        pm = sb(top, "pm", [128, 128], BF16)
        U = sb(top, "U", [128, 128], F32)
        ind = sb(top, "ind", [128, 2], F32)
        gateB = sb(top, "gateB", [128, D], F32)
        wA = sb(top, "wA", [128, 128], F32)
        wB = sb(top, "wB", [128, 256], F32)
        aT = sb(top, "aT", [128, 16], F32)
        shT = sb(top, "shT", [128, 16], F32)
        nwT = sb(top, "nwT", [128, 16], F32)
        neglam = sb(top, "neglam", [128, 1], F32)
        lamb = sb(top, "lamb", [128, 2, 2, 64], F32)
        lamp = sb(top, "lamp", [128, 2, 64], F32)
        lams = sb(top, "lams", [128, 2], F32)
        lame = sb(top, "lame", [128, 2], F32)
        one11 = sb(top, "one11", [1, 1], F32)
        onesrow = sb(top, "onesrow", [1, 128], F32)
        sm = sb(top, "sm", [128, 64], F32)

        P.dma("pool", lambda e: e.dma_start(out=ident[:], in_=ident_d.ap()), [], ["ident"])
        P.dma("pool", lambda e: e.dma_start(out=pm[:], in_=pm_d.ap()), [], ["pm"])
        P.dma("sp", lambda e: e.dma_start(out=U[:], in_=U_d.ap()), [], ["U"])
        P.dma("sp", lambda e: e.dma_start(out=ind[:], in_=ind_d.ap()), [], ["ind"])
        P.dma("sp", lambda e: e.dma_start(out=nwT[:], in_=nwT_d.ap()), [], ["nwT"])
        P.dma("sp", lambda e: e.dma_start(out=wA[:], in_=bcast_rows(dnw_d, 128)), [], ["wA"])
        P.dma("sp", lambda e: e.dma_start(out=wB[:], in_=bcast_rows(gnw_d, 256)), [], ["wB"])
        P.dma("sp", lambda e: e.dma_start(
            out=lamb[:].rearrange("p a b c -> p (a b c)"), in_=bcast_rows(lam4_d, 256)), [], ["lamb"])
        P.op("dve", lambda e: e.memset(one11[:], 1.0), [], ["one11"])
        P.op("dve", lambda e: e.memset(onesrow[:], 1.0), [], ["onesrow"])
        P.op("dve", lambda e: e.tensor_scalar(out=wA[:], in0=wA[:], scalar1=0.8, scalar2=None,
                                              op0=ALU.mult), ["wA"], ["wA"])
        P.op("dve", lambda e: e.tensor_tensor(
            out=lamp[:], in0=lamb[:, :, 0, :], in1=lamb[:, :, 1, :], op=ALU.mult), ["lamb"], ["lamp"])
        P.op("dve", lambda e: e.tensor_reduce(
            out=lams[:], in_=lamp[:], axis=mybir.AxisListType.X, op=ALU.add), ["lamp"], ["lams"])
        P.op("act", lambda e: e.activation(out=lame[:], in_=lams[:], func=AF.Exp), ["lams"], ["lame"])
        P.op("dve", lambda e: e.tensor_tensor(
            out=neglam[:], in0=lame[:, 1:2], in1=lame[:, 0:1], op=ALU.subtract), ["lame"], ["neglam"])
        P.op("dve", lambda e: e.tensor_scalar(out=neglam[:], in0=neglam[:], scalar1=-0.2, scalar2=None,
                                              op0=ALU.add), ["neglam"], ["neglam"])

        sH = None
        sW = None
        try:
            if STOP == -1:
                raise _Stop()
            sH = ExitStack()
            hnT = sb(sH, "hnT", [128, 16, T], BF16)
            pairs = [[0, 1], [2, 3], [4, 5], [6, 7]]
            sW = ExitStack()
            wbuf0_ = sb(sW, "wbuf0", [128, 16, 512], BF16)
            wglr = sb(sW, "wglr", [128, 16, 16], BF16)
            with ExitStack() as s01:
                cT_sb = sb(s01, "cT_sb", [128, 16], F32)
                cact = sb(s01, "cact", [128, 16], BF16)
                wring = [sb(s01, f"wring{i}", [128, 2048], BF16) for i in range(8)]
                badaB = sb(s01, "badaB", [1, 3072], F32)
                modsb = sb(s01, "modsb", [1, 3072], F32)
                modss = sb(s01, "modss", [1, 4096], F32)
                modg = modss
                xb = [sb(s01, f"xb{i}", [128, D], F32) for i in range(2)]
                junk = sb(s01, "junk", [128, D], BF16)
                xh = sb(s01, "xh", [128, 4, D], BF16)
                P.dma("sp", lambda e: e.dma_start(out=cT_sb[:], in_=cT_d.ap()), [], ["cT"])
                P.dma("sp", lambda e: e.dma_start(out=badaB[:], in_=bada_d.ap()), [], ["badaB"])
                def ss_dma(kk):
                    P.dma("pool", lambda e: e.dma_start(
                        out=wring[kk % 8][:], in_=wada_d.ap()[:, kk * 3072:kk * 3072 + 2048]),
                        [], [f"wring{kk % 8}"])

                def gate_dma(g_):
                    P.dma("pool", lambda e: e.dma_start(
                        out=wring[g_][:].rearrange("p (a c) -> p a c", a=2),
                        in_=wada_d.ap().rearrange("p (k c) -> p k c", k=16)[:, 2 * g_:2 * g_ + 2, 2048:3072]),
                        [], [f"wring{g_}"])

                for kk in range(8):
                    ss_dma(kk)
                P.op("act", lambda e: e.activation(out=cact[:], in_=cT_sb[:], func=AF.Silu), ["cT"], ["cact"])

                def front(t):
                    i = t % 4
                    b_ = xb[t % 2]
                    bn = f"xb{t % 2}"
                    c0 = 2 * (t % 8)
                    P.dma("sp", lambda e: e.dma_start(
                        out=b_[:], in_=x_d.ap()[t * 128:(t + 1) * 128, :]), [], [bn])
                    P.op("act", lambda e: e.activation(
                        out=junk[:], in_=b_[:], func=AF.Square, accum_out=sm[:, c0:c0 + 1]),
                        [bn], ["junk", f"sm{c0}"])
                    P.op("act", lambda e: e.activation(
                        out=sm[:, c0 + 1:c0 + 2], in_=sm[:, c0:c0 + 1], func=AF.Ln, scale=1.0 / D, bias=EPS),
                        [f"sm{c0}"], [f"sm{c0 + 1}"])
                    P.op("act", lambda e: e.activation(
                        out=sm[:, c0:c0 + 1], in_=sm[:, c0 + 1:c0 + 2], func=AF.Exp, scale=-0.5),
                        [f"sm{c0 + 1}"], [f"sm{c0}"])
                    P.op("dve", lambda e: e.tensor_scalar(
                        out=xh[:, i, :], in0=b_[:], scalar1=sm[:, c0:c0 + 1], scalar2=None, op0=ALU.mult),
                        [bn, f"sm{c0}"], [f"xh{i}"])

                for t in range(4):
                    front(t)

                for k in range(16):
                    for cg in range(4):
                        P.op("pe", lambda e, k=k, cg=cg: e.matmul(
                            ps[cg][0:1, 0:512], lhsT=cact[:, k:k + 1],
                            rhs=wring[k % 8][:, cg * 512:(cg + 1) * 512],
                            start=(k == 0), stop=(k == 15)), ["cact", f"wring{k % 8}"], [f"ps{cg}"],
                            milestone=(cg == 3))
                    if k + 8 < 16:
                        ss_dma(k + 8)
                for cg in range(4):
                    P.op("dve", lambda e, cg=cg: e.tensor_tensor(
                        out=modsb[0:1, cg * 512:(cg + 1) * 512], in0=ps[cg][0:1, 0:512],
                        in1=badaB[0:1, cg * 512:(cg + 1) * 512], op=ALU.add), [f"ps{cg}", "badaB"], ["modsb"])
                P.dma("sp", lambda e: e.dma_start(out=mss_in.ap(), in_=modsb[0:1, 0:2048]), ["modsb"], ["mss_in"])
                P.cc(lambda e: e.collective_compute(
                    "AllGather", ALU.bypass, replica_groups=pairs, ins=[mss_in.ap()], outs=[mss_out.ap()]),
                    ["mss_in"], ["mss_out"])
                P.dma("sp", lambda e: e.dma_start(
                    out=modss[0:1, :], in_=bass.AP(mss_out, 0, [[0, 1], [1, 4096]])), ["mss_out"], ["modss"])
                for j in range(32):
                    jj = j % 16
                    off = (jj // 8) * 2048 + (j // 16) * 1024 + (jj % 8) * 128
                    P.op("pe", lambda e, j=j, off=off: e.matmul(
                        ps[0][:, j:j + 1], lhsT=modss[0:1, off:off + 128], rhs=one11[0:1, 0:1],
                        start=True, stop=True), ["modss", "one11"], ["ps0"], milestone=(j == 31))
                P.op("dve", lambda e: e.scalar_tensor_tensor(
                    out=aT[:], in0=ps[0][:, 16:32], scalar=1.0, in1=nwT[:], op0=ALU.add, op1=ALU.mult),
                    ["ps0", "nwT"], ["aT"])
                P.op("dve", lambda e: e.tensor_copy(out=shT[:], in_=ps[0][:, 0:16]), ["ps0"], ["shT"])

                tb = [(psT[:, 0:4, :], "psT")]
                for bi_ in (4, 5, 6):
                    tb.append((ps[bi_][:, 0:256].bitcast(BF16).rearrange("p (a b) -> p a b", a=4), f"ps{bi_}"))
                def gate_front():
                    for k in range(16):
                        for cg in range(2):
                            P.op("pe", lambda e, k=k, cg=cg: e.matmul(
                                ps[cg][0:1, 0:512], lhsT=cact[:, k:k + 1],
                                rhs=wring[k // 2][:, (k % 2) * 1024 + cg * 512:(k % 2) * 1024 + (cg + 1) * 512],
                                start=(k == 0), stop=(k == 15)), ["cact", f"wring{k // 2}"], [f"ps{cg}"],
                                milestone=(cg == 1))
                    for cg in range(2):
                        P.op("dve", lambda e, cg=cg: e.tensor_tensor(
                            out=modsb[0:1, 2048 + cg * 512:2048 + (cg + 1) * 512], in0=ps[cg][0:1, 0:512],
                            in1=badaB[0:1, 2048 + cg * 512:2048 + (cg + 1) * 512], op=ALU.add),
                            [f"ps{cg}", "badaB"], ["modsb"])
                    P.dma("sp", lambda e: e.dma_start(out=mg_in.ap(), in_=modsb[0:1, 2048:3072]), ["modsb"], ["mg_in"])
                    P.cc(lambda e: e.collective_compute(
                        "AllGather", ALU.bypass, replica_groups=pairs, ins=[mg_in.ap()], outs=[mg_out.ap()]),
                        ["mg_in"], ["mg_out"])

                ti = 0
                for g_ in range(8):
                    s0_ = 2 * (g_ % 2)
                    for j in range(16):
                        tv, tn = tb[ti % 4]
                        ti += 1
                        for i in range(2):
                            P.op("pe", lambda e, i=i, j=j, tv=tv, s0_=s0_: e.transpose(
                                out=tv[:, i, :], in_=xh[:, s0_ + i, j * 128:(j + 1) * 128],
                                identity=ident[:]), [f"xh{s0_ + i}", "ident"], [tn], milestone=(i == 1))
                        srcv = tv[:, 0:2, :].rearrange("p a b -> p (a b)")
                        if j % 4 == 0:
                            P.op("act", lambda e, j=j, g_=g_, srcv=srcv: e.activation(
                                out=hnT[:, j, g_ * 256:(g_ + 1) * 256], in_=srcv, func=AF.Identity,
                                scale=aT[:, j:j + 1], bias=shT[:, j:j + 1]),
                                [tn, "aT", "shT"], [f"hnT_{g_}_{j}"])
                        else:
                            P.op("dve", lambda e, j=j, g_=g_, srcv=srcv: e.tensor_scalar(
                                out=hnT[:, j, g_ * 256:(g_ + 1) * 256], in0=srcv, scalar1=aT[:, j:j + 1],
                                scalar2=shT[:, j:j + 1], op0=ALU.mult, op1=ALU.add),
                                [tn, "aT", "shT"], [f"hnT_{g_}_{j}"])
                    if g_ + 2 < 8:
                        front(2 * (g_ + 2))
                        front(2 * (g_ + 2) + 1)
                    if 1 <= g_ <= 4:
                        gate_dma(2 * (g_ - 1))
                        gate_dma(2 * (g_ - 1) + 1)
                    if g_ == 5:
                        for kk in range(4):
                            P.dma("pool", lambda e, kk=kk: e.dma_start(
                                out=wbuf0_[:, kk * 4:(kk + 1) * 4, :].rearrange("p k c -> p (k c)"),
                                in_=win_d.ap()[4, :, kk * 2048:(kk + 1) * 2048]), [], ["wbuf0"])
                        P.dma("pool", lambda e: e.dma_start(
                            out=wglr[:].rearrange("p k c -> p (k c)"), in_=wglr_d.ap()), [], ["wglr"])
                    if g_ == 6:
                        gate_front()

                P.dma("sp", lambda e: e.dma_start(
                    out=modg[0:1, 0:2048], in_=bass.AP(mg_out, 0, [[0, 1], [1, 2048]])), ["mg_out"], ["modg", "modss"])
                for n in range(4):
                    P.op("pe", lambda e, n=n: e.matmul(
                        ps[2 + n % 2][:, 0:512], lhsT=onesrow[0:1, 0:128],
                        rhs=modg[0:1, n * 512:(n + 1) * 512], start=True, stop=True),
                        ["modg", "onesrow"], [f"ps{2 + n % 2}"])
                    P.op("act", lambda e, n=n: e.activation(
                        out=gateB[:, n * 512:(n + 1) * 512], in_=ps[2 + n % 2][:, 0:512], func=AF.Copy),
                        [f"ps{2 + n % 2}"], ["gateB"])
                P.barrier()
            if DEBUG:
                P.dma("sp", lambda e: e.dma_start(
                    out=dbg_hn.ap(), in_=hnT[:].rearrange("p a b -> p (a b)")), ["hnTall"], [])

            if STOP in (1, 11, 12):
                raise _Stop()
            wbuf = [wbuf0_, sb(sW, "wbuf1", [128, 16, 512], BF16)]
            ystg = [sb(sW, f"ystg{i}", [128, 256], BF16) for i in range(4)]
            ycnt = [0]

            SEQ = [4, 5, 6, 0, 1, 2, 3]
            WB = {g: i % 2 for i, g in enumerate(SEQ)}

            def after_group(g):
                i = SEQ.index(g)
                if i + 2 < len(SEQ):
                    load_group(SEQ[i + 2])

            def load_group(g):
                wb = wbuf[WB[g]]
                for kk in range(4):
                    P.dma("pool", lambda e, g=g, kk=kk, wb=wb: e.dma_start(
                        out=wb[:, kk * 4:(kk + 1) * 4, :].rearrange("p k c -> p (k c)"),
                        in_=win_d.ap()[g, :, kk * 2048:(kk + 1) * 2048]), [], [f"wbuf{WB[g]}"])

            ipc = [0]
            ipbanks = [0, 1, 2]
            pend = []

            def inproj_block(g, blk, tg, M=128, w=None, wname=None):
                todo = pend[:]
                del pend[:]
                bi = ipbanks[ipc[0] % len(ipbanks)]
                ipc[0] += 1
                pst = ps[bi]
                for k in range(16):
                    if w is None:
                        lhs = lambda k=k: wbuf[WB[g]][:, k, blk * 128:blk * 128 + M]
                        wn = f"wbuf{WB[g]}"
                    else:
                        lhs = lambda k=k: w[:, k, 0:M]
                        wn = wname
                    P.op("pe", lambda e, k=k, lhs=lhs, pst=pst: e.matmul(
                        pst[0:M, 0:512], lhsT=lhs(), rhs=hnT[:, k, tg * 512:(tg + 1) * 512],
                        start=(k == 0), stop=(k == 15)), [wn, "hnT"], [f"ps{bi}"], milestone=(k == 15))
                for f_ in todo:
                    f_()
                return pst, f"ps{bi}"

            def flush_pend():
                todo = pend[:]
                del pend[:]
                for f_ in todo:
                    f_()

            trc = [0]

            def evac_transposed(pst, pname, dst_fn, dname, tg, func, rawbufs, rprefix):
                ri = trc[0] % 2
                trc[0] += 1
                raw = rawbufs[ri]
                rn = f"{rprefix}{ri}"
                P.op("act", lambda e: e.activation(out=raw[:], in_=pst[:, 0:512], func=func), [pname], [rn])
                slot = ri

                def post():
                    for i in range(4):
                        P.op("pe", lambda e, i=i: e.transpose(
                            out=psT[:, slot * 4 + i, :], in_=raw[:, i * 128:(i + 1) * 128], identity=ident[:]),
                            [rn, "ident"], ["psT"], milestone=(i == 3))
                    P.op("dve", lambda e: e.tensor_copy(out=dst_fn(), in_=psT[:, slot * 4:(slot + 1) * 4, :]),
                         ["psT"], [dname])

                pend.append(post)

            def ywrite(src_fn, sname, t, c0, width):
                dst = ya_in if c0 < 512 else yg_in
                cc0 = c0 % 512
                P.dma("sp", lambda e: e.dma_start(
                    out=dst.ap()[t * 128:(t + 1) * 128, cc0:cc0 + width], in_=src_fn()), [sname], [f"y_in_{t}_{c0}"])

            load_group(5)

            with ExitStack() as sG:
                gqT = sb(sG, "gqT", [128, 2, T], BF16)
                gk = sb(sG, "gk", [128, NT, 256], BF16)
                gv = sb(sG, "gv", [128, NT, 512], BF16)
                gg = sb(sG, "gg", [128, NT, 512], BF16)
                glrT = sb(sG, "glrT", [32, T], F32)
                w2aug = sb(sG, "w2aug", [17, 256], F32)
                dec = sb(sG, "dec", [128, 2, 32], F32)
                raws = [sb(sG, f"graw{i}", [128, 512], BF16) for i in range(2)]
                e1 = [sb(sG, f"e1_{i}", [128, 256], F32) for i in range(2)]
                lap = [sb(sG, f"lap{i}", [128, 256], F32) for i in range(2)]
                ed = [sb(sG, f"ed{i}", [128, 256], F32) for i in range(2)]
                sqg = sb(sG, "sqg", [128, 256], F32)
                Sst = [sb(sG, f"Sst{i}", [128, 256], F32) for i in range(2)]
                Sbf = sb(sG, "Sbf", [128, 32, 256], BF16)
                gw = [sb(sG, f"gw{i}", [128, 256], F32) for i in range(2)]

                P.dma("sp", lambda e: e.dma_start(out=w2aug[:], in_=w2aug_d.ap()), [], ["w2aug"])
                P.op("pool", lambda e: e.memset(glrT[:], 1.0), [], ["glrT"])
                for blk in range(4):
                    e_ = blk // 2
                    for tg in range(4):
                        pst, pn = inproj_block(4, blk, tg)
                        if blk % 2 == 0:
                            P.op("act", lambda e, pst=pst, e_=e_, tg=tg: e.activation(
                                out=gqT[:, e_, tg * 512:(tg + 1) * 512], in_=pst[:, 0:512], func=AF.Copy,
                                scale=float(128 ** -0.5)), [pn], ["gqT"])
                        else:
                            evac_transposed(pst, pn, lambda e_=e_, tg=tg: gk[:, tg * 4:(tg + 1) * 4,
                                                                            e_ * 128:(e_ + 1) * 128],
                                            "gk", tg, AF.Copy, raws, "graw")
                after_group(4)
                for blk in range(4):
                    for tg in range(4):
                        pst, pn = inproj_block(5, blk, tg)
                        evac_transposed(pst, pn, lambda blk=blk, tg=tg: gv[:, tg * 4:(tg + 1) * 4,
                                                                           blk * 128:(blk + 1) * 128],
                                        "gv", tg, AF.Copy, raws, "graw")
                after_group(5)
                for blk in range(4):
                    for tg in range(4):
                        pst, pn = inproj_block(6, blk, tg)
                        evac_transposed(pst, pn, lambda blk=blk, tg=tg: gg[:, tg * 4:(tg + 1) * 4,
                                                                           blk * 128:(blk + 1) * 128],
                                        "gg", tg, AF.Silu, raws, "graw")
                after_group(6)
                for tg in range(4):
                    pst, pn = inproj_block(None, 0, tg, M=16, w=wglr, wname="wglr")
                    P.op("act", lambda e, pst=pst, tg=tg: e.activation(
                        out=glrT[0:16, tg * 512:(tg + 1) * 512], in_=pst[0:16, 0:512], func=AF.Copy),
                        [pn], ["glrT"])
                flush_pend()
                def prep_a(t):
                    zb = 3 if t % 2 == 0 else 6
                    e1_, lap_ = e1[t % 2], lap[t % 2]
                    P.op("pe", lambda e: e.matmul(
                        ps[zb][:, 0:256], lhsT=glrT[0:17, t * 128:(t + 1) * 128], rhs=w2aug[0:17, :],
                        start=True, stop=True), ["glrT", "w2aug"], [f"ps{zb}"])
                    P.op("act", lambda e: e.activation(out=e1_[:], in_=ps[zb][:, 0:256], func=AF.Exp, scale=-1.0),
                         [f"ps{zb}"], [f"e1_{t % 2}"])
                    P.op("act", lambda e: e.activation(out=lap_[:], in_=e1_[:], func=AF.Ln, bias=1.0),
                         [f"e1_{t % 2}"], [f"lap{t % 2}"])

                def prep_b(t):
                    lap_, ed_ = lap[t % 2], ed[t % 2]
                    P.op("pe", lambda e: e.matmul(ps[4][:, 0:256], lhsT=U[:], rhs=lap_[:], start=True, stop=True),
                         ["U", f"lap{t % 2}"], ["ps4"])
                    for e_ in range(2):
                        P.op("pe", lambda e, e_=e_: e.matmul(
                            ps[5][:, e_ * 2:(e_ + 1) * 2], lhsT=lap_[:, e_ * 128:(e_ + 1) * 128], rhs=ind[:],
                            start=True, stop=True), [f"lap{t % 2}", "ind"], ["ps5"], milestone=(e_ == 1))
                    P.op("act", lambda e: e.activation(out=ed_[:], in_=ps[4][:, 0:256], func=AF.Exp,
                                                       scale=-1.0 / 16.0), ["ps4"], [f"ed{t % 2}"])
                    P.op("act", lambda e: e.activation(
                        out=dec[:, :, 2 * t:2 * t + 2],
                        in_=ps[5][:, 0:4].rearrange("p (a b) -> p a b", a=2), func=AF.Exp, scale=-1.0 / 16.0),
                        ["ps5"], ["dec"])
                    P.op("dve", lambda e: e.tensor_tensor(
                        out=gk[:, t, :], in0=gk[:, t, :], in1=ed_[:], op=ALU.mult), ["gk", f"ed{t % 2}"], ["gk"])

                prep_a(0)
                for t in range(NT):
                    if t + 1 < NT:
                        prep_a(t + 1)
                    prep_b(t)

                def scan_step(e_, c):
                    t, h = c // 2, c % 2
                    bi = 3 + c % 2
                    P.op("pe", lambda e: e.matmul(
                        ps[bi][:, 0:256], lhsT=gk[64 * h:64 * h + 64, t, e_ * 128:(e_ + 1) * 128],
                        rhs=gv[64 * h:64 * h + 64, t, e_ * 256:(e_ + 1) * 256], start=True, stop=True),
                        ["gk", "gv"], [f"ps{bi}"])
                    sn, so = Sst[(c + 1) % 2], Sst[c % 2]
                    if c == 0:
                        P.op("dve", lambda e: e.tensor_copy(out=sn[:], in_=ps[bi][:, 0:256]),
                             [f"ps{bi}"], [f"Sst{(c + 1) % 2}"])
                    else:
                        P.op("dve", lambda e: e.scalar_tensor_tensor(
                            out=sn[:], in0=so[:], scalar=dec[:, e_, c:c + 1], in1=ps[bi][:, 0:256],
                            op0=ALU.mult, op1=ALU.add),
                            [f"ps{bi}", f"Sst{c % 2}", "dec"], [f"Sst{(c + 1) % 2}"])
                    P.op("dve", lambda e: e.tensor_copy(out=Sbf[:, c, :], in_=sn[:]),
                         [f"Sst{(c + 1) % 2}"], [f"Sbf{c}"])

                def out_tile(e_, t):
                    bi = 5 + t % 2
                    for h in range(2):
                        c = 2 * t + h
                        P.op("pe", lambda e, c=c, h=h: e.matmul(
                            ps[bi][64 * h:64 * h + 64, 0:256], lhsT=gqT[:, e_, c * 64:(c + 1) * 64],
                            rhs=Sbf[:, c, :], start=True, stop=True, tile_position=(0, 64 * h)),
                            ["gqT", f"Sbf{c}"], [f"ps{bi}"], milestone=(h == 1))
                    c0 = 32 + 2 * (t % 8)
                    g_ = gw[t % 2]
                    ys = ystg[ycnt[0] % 4]
                    ysn = f"ystg{ycnt[0] % 4}"
                    ycnt[0] += 1
                    P.op("act", lambda e: e.activation(
                        out=sqg[:], in_=ps[bi][:, 0:256], func=AF.Square, accum_out=sm[:, c0:c0 + 1]),
                        [f"ps{bi}"], ["sqg", f"sm{c0}"])
                    P.op("act", lambda e: e.activation(
                        out=sm[:, c0 + 1:c0 + 2], in_=sm[:, c0:c0 + 1], func=AF.Ln, scale=1.0 / 256, bias=EPS),
                        [f"sm{c0}"], [f"sm{c0 + 1}"])
                    P.op("act", lambda e: e.activation(
                        out=sm[:, c0:c0 + 1], in_=sm[:, c0 + 1:c0 + 2], func=AF.Exp, scale=-0.5),
                        [f"sm{c0 + 1}"], [f"sm{c0}"])
                    P.op("pool", lambda e: e.tensor_tensor(
                        out=g_[:], in0=gg[:, t, e_ * 256:(e_ + 1) * 256], in1=wB[:], op=ALU.mult),
                        ["gg", "wB"], [f"gw{t % 2}"])
                    P.op("dve", lambda e: e.scalar_tensor_tensor(
                        out=ys[:], in0=ps[bi][:, 0:256], scalar=sm[:, c0:c0 + 1], in1=g_[:],
                        op0=ALU.mult, op1=ALU.mult), [f"ps{bi}", f"sm{c0}", f"gw{t % 2}"], [ysn])
                    ywrite(lambda: ys[:], ysn, t, 512 + e_ * 256, 256)

                for c in range(32):
                    scan_step(0, c)
                for t in range(NT):
                    out_tile(0, t)
                    scan_step(1, 2 * t)
                    scan_step(1, 2 * t + 1)
                for t in range(NT):
                    out_tile(1, t)
                P.barrier()

            pairs = [[0, 1], [2, 3], [4, 5], [6, 7]]
            P.cc(lambda e: e.collective_compute(
                "AllGather", ALU.bypass, replica_groups=pairs, ins=[yg_in.ap()], outs=[yg_out.ap()]),
                [], ["yg_out"])
            if STOP == 2:
                raise _Stop()
            with ExitStack() as sA:
                cosT = sb(sA, "cosT", [128, T], F32)
                sinT = sb(sA, "sinT", [128, T], F32)
                qT2 = [sb(sA, f"qT{i}", [128, T], BF16) for i in range(2)]
                kT2 = [[sb(sA, f"kT{i}_{c}", [128, T], BF16) for c in range(2)] for i in range(2)]
                Vp2 = [sb(sA, f"Vp{i}", [128, NT, 130], BF16) for i in range(2)]
                Gs2 = [sb(sA, f"Gs{i}", [128, NT, 128], BF16) for i in range(2)]
                raws = [sb(sA, f"araw{i}", [128, 512], BF16) for i in range(2)]
                rraw = [sb(sA, f"rraw{i}", [128, 512], BF16) for i in range(2)]
                t1 = [sb(sA, f"t1_{i}", [128, 512], F32) for i in range(2)]
                t2 = [sb(sA, f"t2_{i}", [128, 512], F32) for i in range(2)]
                NPT = 12
                pT = [sb(sA, f"pT{i}", [128, 2, 2, 128], BF16) for i in range(NPT)]
                o1 = [sb(sA, f"o1_{i}", [128, 128], F32) for i in range(6)]
                o2 = [sb(sA, f"o2_{i}", [128, 128], F32) for i in range(6)]
                gwa = [sb(sA, f"gwa{i}", [128, 128], F32) for i in range(6)]
                smE = sb(sA, "smE", [128, 48], F32)
                epc = [0]
                sqj = sb(sA, "sqj", [128, 128], F32)
                psTf = psT[:].rearrange("p a b -> p (a b)").bitcast(F32)
                P.dma("sp", lambda e: e.dma_start(out=cosT[:], in_=cos_d.ap()), [], ["cosT"])
                P.dma("sp", lambda e: e.dma_start(out=sinT[:], in_=sin_d.ap()), [], ["sinT"])
                for b_ in range(2):
                    P.op("pool", lambda e, b_=b_: e.memset(Vp2[b_][:], 1.0), [], [f"Vp{b_}"])
                    P.op("pool", lambda e, b_=b_: e.memset(kT2[b_][0][64:128, :], 0.0), [], [f"kT{b_}"])
                    P.op("pool", lambda e, b_=b_: e.memset(kT2[b_][1][0:64, :], 0.0), [], [f"kT{b_}"])
                rc = [0]
                pc = [0]
                sc = [0]
                del ipbanks[:]
                ipbanks.extend([0, 1])

                def inproj_closures(hd):
                    bb = hd % 2
                    out = []
                    for blk in range(4):
                        for tg in range(4):
                            def blockfn(blk=blk, tg=tg):
                                todo = pend[:]
                                del pend[:]
                                bi = ipbanks[ipc[0] % len(ipbanks)]
                                ipc[0] += 1
                                pst, pn = ps[bi], f"ps{bi}"
                                for k in range(16):
                                    P.op("pe", lambda e, k=k: e.matmul(
                                        pst[:, 0:512], lhsT=wbuf[WB[hd]][:, k, blk * 128:(blk + 1) * 128],
                                        rhs=hnT[:, k, tg * 512:(tg + 1) * 512], start=(k == 0), stop=(k == 15)),
                                        [f"wbuf{WB[hd]}", "hnT"], [pn], milestone=(k == 15))
                                    if k % 4 == 3 and k < 15:
                                        yield
                                for f_ in todo:
                                    f_()
                                if blk < 2:
                                    ri = rc[0] % 2
                                    rc[0] += 1
                                    rr_, a1, a2 = rraw[ri], t1[ri], t2[ri]
                                    P.op("act", lambda e: e.activation(
                                        out=rr_[:], in_=pst[:, 0:512], func=AF.Copy), [pn], [f"rraw{ri}"])

                                    def rpost():
                                        P.op("pe", lambda e: e.matmul(
                                            psTf, lhsT=pm[:], rhs=rr_[:], start=True, stop=True),
                                            ["pm", f"rraw{ri}"], ["psT"])
                                        P.op("dve", lambda e: e.tensor_tensor(
                                            out=a1[:], in0=rr_[:], in1=cosT[:, tg * 512:(tg + 1) * 512],
                                            op=ALU.mult), [f"rraw{ri}", "cosT"], [f"t1_{ri}"])
                                        P.op("dve", lambda e: e.tensor_tensor(
                                            out=a2[:], in0=psTf, in1=sinT[:, tg * 512:(tg + 1) * 512],
                                            op=ALU.mult), ["psT", "sinT"], [f"t2_{ri}"])
                                        if blk == 0:
                                            P.op("dve", lambda e: e.tensor_tensor(
                                                out=qT2[bb][:, tg * 512:(tg + 1) * 512], in0=a1[:], in1=a2[:],
                                                op=ALU.add), [f"t1_{ri}", f"t2_{ri}"], [f"qT{bb}"])
                                        else:
                                            for c in range(2):
                                                P.op("dve", lambda e, c=c: e.tensor_tensor(
                                                    out=kT2[bb][c][64 * c:64 * c + 64, tg * 512:(tg + 1) * 512],
                                                    in0=a1[64 * c:64 * c + 64, :], in1=a2[64 * c:64 * c + 64, :],
                                                    op=ALU.add), [f"t1_{ri}", f"t2_{ri}"], [f"kT{bb}"])

                                    pend.append(rpost)
                                elif blk == 2:
                                    evac_transposed(pst, pn, lambda: Vp2[bb][:, tg * 4:(tg + 1) * 4, 0:128],
                                                    f"Vp{bb}", tg, AF.Copy, raws, "araw")
                                else:
                                    evac_transposed(pst, pn, lambda: Gs2[bb][:, tg * 4:(tg + 1) * 4, :],
                                                    f"Gs{bb}", tg, AF.Silu, raws, "araw")
                            out.append(blockfn)
                    return out

                for g_ in inproj_closures(0):
                    for _ in g_():
                        pass
                after_group(0)
                for hd in range(4):
                    bb = hd % 2
                    qT, kTc, Vp, Gs = qT2[bb], kT2[bb], Vp2[bb], Gs2[bb]
                    flush_pend()
                    fill = inproj_closures(hd + 1) if hd < 3 else []
                    steps = [(qi, [k_ for k_ in (kp, kp + 1) if k_ <= qi])
                             for qi in range(NT) for kp in range(0, qi + 1, 2)]
                    burst = {}

                    def acc_of(qi):
                        bank = 5 + qi % 2
                        return [ps[bank][:, 0:129], ps[bank][:, 256:385]], f"ps{bank}"

                    def emit_qk(qi, kis, qT=qT, kTc=kTc, bb=bb):
                        sbi = 2 + sc[0] % 3
                        sc[0] += 1
                        psS = ps[sbi][:, 0:512].rearrange("p (j c q) -> p j c q", j=2, c=2)
                        for j, ki in enumerate(kis):
                            for c in range(2):
                                P.op("pe", lambda e, j=j, ki=ki, c=c: e.matmul(
                                    psS[:, j, c, :], lhsT=kTc[c][:, ki * 128:(ki + 1) * 128],
                                    rhs=qT[:, qi * 128:(qi + 1) * 128], start=True, stop=True),
                                    [f"kT{bb}", f"qT{bb}"], [f"ps{sbi}"],
                                    milestone=(j == len(kis) - 1 and c == 1))
                        pi = pc[0] % NPT
                        pc[0] += 1
                        pt = pT[pi]
                        nk = len(kis)
                        P.op("act", lambda e: e.activation(
                            out=pt[:, 0:nk], in_=psS[:, 0:nk], func=AF.Exp, scale=0.125),
                            [f"ps{sbi}"], [f"pT{pi}"])
                        if qi in kis:
                            j = kis.index(qi)
                            P.op("dve", lambda e, j=j: e.memset(pt[64:128, j, :, 0:64], 0.0), [], [f"pT{pi}"])
                        return (qi, kis, pt, pi)

                    def emit_av(qi, kis, pt, pi, Vp=Vp, bb=bb):
                        accs, accn = acc_of(qi)
                        for j, ki in enumerate(kis):
                            P.op("pe", lambda e, j=j, ki=ki: e.matmul(
                                accs[0], lhsT=pt[:, j, 0, :], rhs=Vp[:, ki, 0:129],
                                start=(ki == 0), stop=(ki == qi)), [f"pT{pi}", f"Vp{bb}"], [accn],
                                milestone=(j == len(kis) - 1))
                            burst.setdefault(qi, []).append((pt, pi, j, ki))
                        if kis[-1] == qi:
                            items = burst.pop(qi)
                            for n_, (pt_, pi_, j_, ki_) in enumerate(items):
                                P.op("pe", lambda e, pt_=pt_, j_=j_, ki_=ki_, n_=n_: e.matmul(
                                    accs[1], lhsT=pt_[:, j_, 1, :], rhs=Vp[:, ki_, 0:129],
                                    start=(n_ == 0), stop=(n_ == len(items) - 1)), [f"pT{pi_}", f"Vp{bb}"],
                                    [accn], milestone=(n_ == len(items) - 1))
                            emit_epilogue(qi)

                    def emit_epilogue(qi, hd=hd, Gs=Gs, bb=bb):
                        accs, accn = acc_of(qi)
                        r6 = epc[0] % 6
                        epc[0] += 1
                        oa, ob, g_ = o1[r6], o2[r6], gwa[r6]
                        smn = f"smE{r6}"
                        b0 = 8 * r6
                        for c in range(2):
                            P.op("dve", lambda e, c=c: e.reciprocal(
                                out=smE[:, b0 + c:b0 + c + 1], in_=accs[c][:, 128:129]), [accn], [smn])
                        P.op("dve", lambda e: e.tensor_tensor(
                            out=smE[:, b0 + 2:b0 + 3], in0=smE[:, b0 + 1:b0 + 2], in1=neglam[:], op=ALU.mult),
                            [smn, "neglam"], [smn])
                        P.op("dve", lambda e: e.tensor_scalar(
                            out=oa[:], in0=accs[0][:, 0:128], scalar1=smE[:, b0:b0 + 1], scalar2=None,
                            op0=ALU.mult), [accn, smn], [f"o1_{r6}"])
                        P.op("dve", lambda e: e.scalar_tensor_tensor(
                            out=ob[:], in0=accs[1][:, 0:128], scalar=smE[:, b0 + 2:b0 + 3], in1=oa[:],
                            op0=ALU.mult, op1=ALU.add), [accn, smn, f"o1_{r6}"], [f"o2_{r6}"])
                        P.op("dve", lambda e: e.tensor_tensor(
                            out=g_[:], in0=Gs[:, qi, :], in1=wA[:], op=ALU.mult), [f"Gs{bb}", "wA"], [f"gwa{r6}"])
                        P.op("dve", lambda e: e.scalar_tensor_tensor(
                            out=sqj[:], in0=ob[:], scalar=1.0, in1=ob[:], op0=ALU.mult, op1=ALU.mult,
                            accum_out=smE[:, b0 + 3:b0 + 4]), [f"o2_{r6}"], ["sqj", smn + "s"])

                        def stage_b():
                            P.op("act", lambda e: e.activation(
                                out=smE[:, b0 + 4:b0 + 5], in_=smE[:, b0 + 3:b0 + 4], func=AF.Ln,
                                scale=1.0 / 128, bias=EPS), [smn + "s"], [smn + "t"])
                            P.op("act", lambda e: e.activation(
                                out=smE[:, b0 + 5:b0 + 6], in_=smE[:, b0 + 4:b0 + 5], func=AF.Exp, scale=-0.5),
                                [smn + "t"], [smn + "u"])
                            deferred.append([2, stage_c])

                        def stage_c():
                            ys = ystg[ycnt[0] % 4]
                            ysn = f"ystg{ycnt[0] % 4}"
                            ycnt[0] += 1
                            P.op("dve", lambda e: e.scalar_tensor_tensor(
                                out=ys[:, 0:128], in0=ob[:], scalar=smE[:, b0 + 5:b0 + 6], in1=g_[:],
                                op0=ALU.mult, op1=ALU.mult), [f"o2_{r6}", smn + "u", f"gwa{r6}"], [ysn])
                            ywrite(lambda: ys[:, 0:128], ysn, qi, hd * 128, 128)

                        deferred.append([4, stage_b])

                    def tick():
                        for d_ in deferred:
                            d_[0] -= 1
                        while deferred and deferred[0][0] <= 0:
                            deferred.pop(0)[1]()

                    curgen = [None]

                    def advance_fill():
                        if curgen[0] is None:
                            if not fill:
                                return
                            curgen[0] = fill.pop(0)()
                        try:
                            next(curgen[0])
                        except StopIteration:
                            curgen[0] = None

                    deferred = []
                    inflight = []
                    for si, (qi, kis) in enumerate(steps):
                        tick()
                        inflight.append(emit_qk(qi, kis))
                        if len(inflight) > 2:
                            emit_av(*inflight.pop(0))
                        for _ in range(FILL_BURST if si % FILL_EVERY == 1 else 0):
                            advance_fill()
                    while inflight:
                        tick()
                        emit_av(*inflight.pop(0))
                    while deferred:
                        deferred.pop(0)[1]()
                    while fill or curgen[0] is not None:
                        advance_fill()
                    if hd < 3:
                        after_group(hd + 1)
                    if hd == 2 and STOP > 37:
                        flush_pend()
                        for kk in range(8):
                            P.dma("pool", lambda e, kk=kk: e.dma_start(
                                out=hnT[:, kk * 2:(kk + 1) * 2, :].rearrange("p k c -> p (k c)"),
                                in_=wout_d.ap()[:, kk * 4096:(kk + 1) * 4096]),
                                ([] if kk == 0 else ["hnT"]), (["hnT", "wout0"] if kk == 0 else [f"wout{kk}"]))
                P.barrier()
            sW.close()
            sW = None

            if STOP in (3, 31, 32, 33, 34, 35, 36, 37):
                raise _Stop()
            P.cc(lambda e: e.collective_compute(
                "AllGather", ALU.bypass, replica_groups=pairs, ins=[ya_in.ap()], outs=[ya_out.ap()]),
                [], ["ya_out"])
            if DEBUG:
                P.dma("sp", lambda e: e.dma_start(out=dbg_y.ap()[:, 0:512], in_=ya_in.ap()), [], [])
                P.dma("sp", lambda e: e.dma_start(out=dbg_y.ap()[:, 512:1024], in_=yg_in.ap()), [], [])

            if STOP == 4:
                raise _Stop()
            with ExitStack() as sO:
                wout = hnT
                fnwB = sb(sO, "fnwB", [128, D], F32)
                Yt = [sb(sO, f"Yt{i}", [128, 1024], BF16) for i in range(3)]
                yT = [sb(sO, f"yT{i}", [128, 8, 128], BF16) for i in range(2)]
                xo = [sb(sO, f"xo{i}", [128, D], F32) for i in range(3)]
                hb = [sb(sO, f"hb{i}", [128, D], F32) for i in range(8)]
                tmpm = [sb(sO, f"tmpm{i}", [128, 512], F32) for i in range(2)]
                junk2 = sb(sO, "junk2", [128, D], BF16)
                P.dma("sp", lambda e: e.dma_start(out=fnwB[:], in_=bcast_rows(fnw_d, D)), [], ["fnwB"])
                ycn = [0]

                def rank_of(e):
                    if "rank" not in PID:
                        PID["rank"] = e.partition_id() % 2
                    return PID["rank"]

                def loads(i, other):
                    slot = ycn[0] % 3
                    ycn[0] += 1
                    Y = Yt[slot]
                    for piece, (loc_t, gat_t, cname) in enumerate(((ya_in, ya_out, "ya_out"),
                                                                   (yg_in, yg_out, "yg_out"))):
                        def ld(e, i=i, Y=Y, piece=piece, loc_t=loc_t, gat_t=gat_t):
                            rank = rank_of(e)
                            if other:
                                src = gat_t.ap().rearrange("(q i p) c -> q i p c", q=4, i=8)
                                sel = src[ds(2 - rank, 1), i, :, :]
                            else:
                                src = loc_t.ap().rearrange("(h i p) c -> h i p c", h=2, i=8)
                                sel = src[ds(rank, 1), i, :, :]
                            return e.dma_start(out=Y[:, piece * 512:(piece + 1) * 512],
                                               in_=sel.rearrange("a p c -> (a p) c"))
                        P.dma("sp", ld, ([cname] if other else []), [f"Yt{slot}_{piece}"])
                    return slot

                def tchunk(slot, tslot, n):
                    Y, yt_ = Yt[slot], yT[tslot]
                    for a_ in range(4):
                        k = 4 * n + a_
                        P.op("pe", lambda e, a_=a_, k=k, Y=Y: e.transpose(
                            out=psT[:, a_, :], in_=Y[:, k * 128:(k + 1) * 128], identity=ident[:]),
                            [f"Yt{slot}_{k // 4}", "ident"], ["psT"], milestone=(a_ == 3))
                    P.op("act", lambda e, n=n, yt_=yt_: e.activation(
                        out=yt_[:, 4 * n:4 * n + 4, :], in_=psT[:, 0:4, :], func=AF.Copy),
                        ["psT"], [f"yT{tslot}"])

                def xload(i):
                    P.dma("sp", lambda e: e.dma_start(
                        out=xo[i % 3][:], in_=xo_d.ap()[i * 128:(i + 1) * 128, :]), [], [f"xo{i % 3}"])

                def scale_chunk(k):
                    P.op("dve", lambda e: e.tensor_tensor(
                        out=wout[:, k, :], in0=wout[:, k, :], in1=gateB[:], op=ALU.mult),
                        [f"wout{k // 2}", "gateB"], [f"wout{k // 2}"])

                for k in range(8):
                    scale_chunk(k)
                for phase in range(2):
                    other = phase == 1
                    if other:
                        for k in range(8, 16):
                            scale_chunk(k)
                    slots = {}
                    slots[0] = loads(0, other)
                    slots[1] = loads(1, other)
                    if not other:
                        xload(0)
                        xload(1)
                    for n in range(2):
                        tchunk(slots[0], 0, n)
                    for i in range(8):
                        yt_, hb_ = yT[i % 2], hb[i]
                        for n in range(4):
                            bi = n
                            for k in range(8):
                                kk = 8 * phase + k
                                P.op("pe", lambda e, k=k, kk=kk, n=n, bi=bi, yt_=yt_: e.matmul(
                                    ps[bi][:, 0:512], lhsT=yt_[:, k, :], rhs=wout[:, kk, n * 512:(n + 1) * 512],
                                    start=(k == 0), stop=(k == 7)), [f"yT{i % 2}", f"wout{kk // 2}"],
                                    [f"ps{bi}"], milestone=(k == 7))
                            if i + 1 < 8 and n < 2:
                                tchunk(slots[i + 1], (i + 1) % 2, n)
                            if not other:
                                xo_ = xo[i % 3]
                                P.op("dve", lambda e, bi=bi, n=n, xo_=xo_, hb_=hb_: e.tensor_tensor(
                                    out=hb_[:, n * 512:(n + 1) * 512], in0=ps[bi][:, 0:512],
                                    in1=xo_[:, n * 512:(n + 1) * 512], op=ALU.add),
                                    [f"ps{bi}", f"xo{i % 3}"], [f"hb{i}_{n}"])
                            else:
                                P.op("dve", lambda e, bi=bi, n=n, hb_=hb_: e.tensor_tensor(
                                    out=hb_[:, n * 512:(n + 1) * 512], in0=ps[bi][:, 0:512],
                                    in1=hb_[:, n * 512:(n + 1) * 512], op=ALU.add),
                                    [f"ps{bi}", f"hb{i}_{n}"], [f"hb{i}_{n}"])
                        if i + 2 < 8:
                            slots[i + 2] = loads(i + 2, other)
                            if not other:
                                xload(i + 2)
                        if other:
                            c0 = 2 * i
                            ob_ = xo[i % 3]
                            hn4 = [f"hb{i}_{n}" for n in range(4)]
                            P.op("act", lambda e, hb_=hb_, c0=c0: e.activation(
                                out=junk2[:], in_=hb_[:], func=AF.Square, accum_out=sm[:, c0:c0 + 1]),
                                hn4, ["junk2", f"sm{c0}"])
                            P.op("act", lambda e, c0=c0: e.activation(
                                out=sm[:, c0 + 1:c0 + 2], in_=sm[:, c0:c0 + 1], func=AF.Ln, scale=1.0 / D,
                                bias=EPS), [f"sm{c0}"], [f"sm{c0 + 1}"])
                            P.op("act", lambda e, c0=c0: e.activation(
                                out=sm[:, c0:c0 + 1], in_=sm[:, c0 + 1:c0 + 2], func=AF.Exp, scale=-0.5),
                                [f"sm{c0 + 1}"], [f"sm{c0}"])
                            P.op("dve", lambda e, hb_=hb_, ob_=ob_, c0=c0: e.scalar_tensor_tensor(
                                out=ob_[:], in0=hb_[:], scalar=sm[:, c0:c0 + 1], in1=fnwB[:], op0=ALU.mult,
                                op1=ALU.mult), hn4 + [f"sm{c0}", "fnwB"], [f"xo{i % 3}"])
                            P.dma("sp", lambda e, i=i, ob_=ob_: e.dma_start(
                                out=out_d.ap()[i * 128:(i + 1) * 128, :], in_=ob_[:]), [f"xo{i % 3}"],
                                ["out%d" % i])
                P.barrier()
            sH.close()
            sH = None
        except _Stop:
            P.barrier()
            for st_ in (sW, sH):
                if st_ is not None:
                    st_.close()

        with nc.Block() as block:
            @block.sync
            def _(e):
                for f in P.q["sp"]:
                    f(e)

            @block.scalar
            def _(e):
                for f in P.q["act"]:
                    f(e)

            @block.vector
            def _(e):
                for f in P.q["dve"]:
                    f(e)

            @block.gpsimd
            def _(e):
                for f in P.q["pool"]:
                    f(e)

            @block.tensor
            def _(e):
                for f in P.q["pe"]:
                    f(e)
    return nc


def _chunked(w):
    C = w.shape[1]
    return np.ascontiguousarray(w.reshape(16, 128, C).transpose(1, 0, 2).reshape(128, 16 * C))


def _consts():
    ident = np.eye(128, dtype=np.float32)
    pm = np.zeros((128, 128), np.float32)
    for m in range(128):
        partner = m + 32 if (m % 64) < 32 else m - 32
        pm[partner, m] = 1.0
    U = np.zeros((128, 128), np.float32)
    for j in range(128):
        for t in range(128):
            if j > t and j // 64 == t // 64:
                U[j, t] = 1.0
    ind = np.zeros((128, 2), np.float32)
    ind[:64, 0] = 1.0
    ind[64:, 1] = 1.0
    pos = np.arange(T, dtype=np.float32)
    inv_freq = (np.float32(10000.0) ** (-np.arange(0, 64, 2, dtype=np.float32) / np.float32(64))).astype(np.float32)
    ang = (pos[None, :] * inv_freq[:, None]).astype(np.float32)
    cos32 = np.cos(ang).astype(np.float32)
    sin32 = np.sin(ang).astype(np.float32)
    cosT = np.zeros((128, T), np.float32)
    sinT = np.zeros((128, T), np.float32)
    for p in range(128):
        cosT[p] = cos32[p % 32]
        sinT[p] = -sin32[p % 32] if (p % 64) < 32 else sin32[p % 32]
    return dict(ident=ident, pm=pm, U=U, ind=ind, cosT=cosT, sinT=sinT)


def _prepare_in_maps(x, c, norm_w, w_ada, b_ada, w_in, lambda_q1, lambda_k1, lambda_q2, lambda_k2,
                     diff_norm_w, gla_gate_w2, gla_gate_b, gla_norm_w, w_out, final_norm_w):
    f = lambda a: np.asarray(a, dtype=np.float32)
    x, c = f(x), f(c)
    w_ada, b_ada, w_in, w_out = f(w_ada)[0], f(b_ada)[0], f(w_in)[0], f(w_out)[0]
    consts = _consts()
    nwT = np.ascontiguousarray(f(norm_w)[0].reshape(16, 128).T)
    lam4 = np.stack([f(lambda_q1)[0], f(lambda_k1)[0], f(lambda_q2)[0], f(lambda_k2)[0]], 0)
    w2 = f(gla_gate_w2)[0]
    gb = f(gla_gate_b)[0]
    O = [0, 1024, 2048, 3072, 4096, 4608, 5120, 6144, 7168]
    in_maps = []
    for core in range(8):
        b, g = core // 2, core % 2
        groups = []
        for hd in range(4):
            H = 4 * g + hd
            cols = [w_in[:, O[i] + H * 128: O[i] + (H + 1) * 128] for i in range(4)]
            groups.append(np.concatenate(cols, 1))
        e0, e1 = 2 * g, 2 * g + 1
        groups.append(np.concatenate([w_in[:, O[4] + e0 * 128:O[4] + (e0 + 1) * 128],
                                      w_in[:, O[5] + e0 * 128:O[5] + (e0 + 1) * 128],
                                      w_in[:, O[4] + e1 * 128:O[4] + (e1 + 1) * 128],
                                      w_in[:, O[5] + e1 * 128:O[5] + (e1 + 1) * 128]], 1))
        groups.append(w_in[:, O[6] + e0 * 256:O[6] + (e1 + 1) * 256])
        groups.append(w_in[:, O[7] + e0 * 256:O[7] + (e1 + 1) * 256])
        win = np.stack([_chunked(gr) for gr in groups], 0)
        wglr = _chunked(w_in[:, O[8]:O[8] + 16])
        w2aug = np.concatenate([w2[:, e0 * 128:(e1 + 1) * 128], gb[None, e0 * 128:(e1 + 1) * 128]], 0)
        o_ = 1 - g
        perm = np.concatenate([np.arange(g * 512, (g + 1) * 512), np.arange(1024 + g * 512, 1024 + (g + 1) * 512),
                               np.arange(o_ * 512, (o_ + 1) * 512),
                               np.arange(1024 + o_ * 512, 1024 + (o_ + 1) * 512)])
        wout = _chunked(w_out[perm, :])
        m = dict(
            x=np.ascontiguousarray(x[b]),
            xo=np.ascontiguousarray(x[b, g * 1024:(g + 1) * 1024]),
            cT=np.ascontiguousarray(c[b].reshape(16, 128).T),
            wada=_chunked(np.concatenate([w_ada[:, j * 2048 + g * 1024:j * 2048 + (g + 1) * 1024]
                                          for j in range(3)], 1)),
            bada=np.ascontiguousarray(np.concatenate(
                [b_ada[j * 2048 + g * 1024:j * 2048 + (g + 1) * 1024] for j in range(3)])[None, :]),
            nwT=nwT, win=win, wglr=wglr, w2aug=np.ascontiguousarray(w2aug), lam4=np.ascontiguousarray(lam4),
            dnw=f(diff_norm_w)[0][None, :].copy(), gnw=f(gla_norm_w)[0][None, :].copy(),
            fnw=f(final_norm_w)[None, :].copy(), wout=wout,
        )
        m.update(consts)
        in_maps.append(m)
    return in_maps


_NC_CACHE = {}


def kernel(**inputs):
    in_maps = _prepare_in_maps(**inputs)
    if "nc" not in _NC_CACHE:
        _NC_CACHE["nc"] = build_program()
    nc = _NC_CACHE["nc"]
    res = run_bass_kernel_spmd(nc, in_maps, core_ids=list(range(8)))
    out = np.zeros((4, T, D), np.float32)
    for core in range(8):
        b, g = core // 2, core % 2
        out[b, g * 1024:(g + 1) * 1024] = np.asarray(res.results[core]["out"], dtype=np.float32)
    if DEBUG:
        kernel.last = res
    return out
```

```python
import os
from contextlib import ExitStack

import numpy as np
import concourse.bass as bass
import concourse.mybir as mybir
from concourse.bass_utils import run_bass_kernel_spmd

F32 = mybir.dt.float32
BF16 = mybir.dt.bfloat16
AF = mybir.ActivationFunctionType
ALU = mybir.AluOpType
ds = bass.ds

T = 2048
D = 2048
NT = 16
EPS = 1e-6
ENGS = ["pe", "act", "dve", "pool", "sp"]
NSLOT = {"sp": 24, "pool": 12}
SEMKEYS = (["pe", "act", "dve", "pool", "cc0", "cc1", "cc2", "cc3"]
           + [f"dq_{q}{i}" for q, n in NSLOT.items() for i in range(n)])
DEBUG = bool(int(os.environ.get("KDEBUG", "0")))
STOP = int(os.environ.get("KSTOP", "99"))
KEV = os.environ.get("KEV", "")
FILL_BURST = int(os.environ.get("KFB", "2"))
FILL_EVERY = int(os.environ.get("KFE", "2"))


class _Stop(Exception):
    pass


class Prog:
    def __init__(self, sems):
        self.sem = sems
        self.q = {e: [] for e in ENGS}
        self.cnt = {k: 0 for k in SEMKEYS}
        self.waited = {e: {k: 0 for k in SEMKEYS} for e in ENGS}
        self.res = {}
        self.dnext = {q: 0 for q in NSLOT}
        self.ncc = 0

    def _deps(self, reads, writes):
        deps = {}

        def add(k, v):
            if deps.get(k, 0) < v:
                deps[k] = v

        for r in reads:
            st = self.res.get(r)
            if st and st["w"]:
                add(*st["w"])
        for w in writes:
            st = self.res.get(w)
            if st:
                if st["w"]:
                    add(*st["w"])
                for k, v in st["r"].items():
                    add(k, v)
        return deps

    def _commit(self, reads, writes, ev):
        for r in reads:
            st = self.res.setdefault(r, {"w": None, "r": {}})
            if st["r"].get(ev[0], 0) < ev[1]:
                st["r"][ev[0]] = ev[1]
        for w in writes:
            self.res[w] = {"w": ev, "r": {}}

    def _waits(self, eng, deps):
        for k, v in deps.items():
            if k == "pe":
                assert v <= self.cnt["pe"], "unresolved PE milestone"
            if self.waited[eng][k] < v:
                self.waited[eng][k] = v
                sem = self.sem[k]
                self.q[eng].append(lambda e, sem=sem, v=v: e.wait_ge(sem, v))

    def op(self, eng, fn, reads=(), writes=(), milestone=True):
        deps = self._deps(reads, writes)
        if eng == "pe":
            deps.pop("pe", None)
        self._waits(eng, deps)
        if eng == "pe" and not milestone:
            ev = ("pe", self.cnt["pe"] + 1)
            self.q[eng].append(fn)
        else:
            self.cnt[eng] += 1
            ev = (eng, self.cnt[eng])
            self.waited[eng][eng] = max(self.waited[eng][eng], 0)
            sem = self.sem[eng]
            self.q[eng].append(lambda e, fn=fn, sem=sem: fn(e).then_inc(sem, 1))
        self._commit(reads, writes, ev)

    def dma(self, qeng, fn, reads=(), writes=()):
        deps = self._deps(reads, writes)
        k = f"dq_{qeng}{self.dnext[qeng] % NSLOT[qeng]}"
        self.dnext[qeng] += 1
        if self.cnt[k] > 0 and deps.get(k, 0) < self.cnt[k]:
            deps[k] = self.cnt[k]
        self._waits(qeng, deps)
        self.cnt[k] += 16
        ev = (k, self.cnt[k])
        sem = self.sem[k]
        self.q[qeng].append(lambda e, fn=fn, sem=sem: fn(e).then_inc(sem, 16))
        self._commit(reads, writes, ev)

    def cc(self, fn, reads=(), writes=()):
        deps = self._deps(reads, writes)
        self._waits("pool", deps)
        k = f"cc{self.ncc}"
        self.ncc += 1
        self.cnt[k] += 1
        ev = (k, self.cnt[k])
        sem = self.sem[k]
        self.q["pool"].append(lambda e, fn=fn, sem=sem: fn(e).then_inc(sem, 1))
        self._commit(reads, writes, ev)

    def barrier(self, keep=()):
        kept = {n: self.res[n] for n in keep if n in self.res}
        skip = {st["w"][0] for st in kept.values() if st["w"] and st["w"][0].startswith("dq_")}
        tgt = {k: v for k, v in self.cnt.items() if k not in skip}
        for eng in ENGS:
            self._waits(eng, dict(tgt))
        self.res = kept


def build_program():
    nc = bass.Bass("TRN2", target_bir_lowering=False)

    def din(name, shape, dt=F32):
        return nc.dram_tensor(name, shape, dt, kind="ExternalInput")

    x_d = din("x", [T, D])
    xo_d = din("xo", [1024, D])
    cT_d = din("cT", [128, 16])
    wada_d = din("wada", [128, 16 * 3072])
    bada_d = din("bada", [1, 3072])
    nwT_d = din("nwT", [128, 16])
    win_d = din("win", [7, 128, 16 * 512])
    wglr_d = din("wglr", [128, 16 * 16])
    w2aug_d = din("w2aug", [17, 256])
    lam4_d = din("lam4", [4, 64])
    dnw_d = din("dnw", [1, 128])
    gnw_d = din("gnw", [1, 256])
    fnw_d = din("fnw", [1, D])
    wout_d = din("wout", [128, 16 * D])
    ident_d = din("ident", [128, 128])
    pm_d = din("pm", [128, 128])
    U_d = din("U", [128, 128])
    ind_d = din("ind", [128, 2])
    cos_d = din("cosT", [128, T])
    sin_d = din("sinT", [128, T])
    out_d = nc.dram_tensor("out", [1024, D], F32, kind="ExternalOutput")
    mss_in = nc.dram_tensor("mss_in", [1, 2048], F32)
    mss_out = nc.dram_tensor("mss_out", [2, 2048], F32)
    mg_in = nc.dram_tensor("mg_in", [1, 1024], F32)
    mg_out = nc.dram_tensor("mg_out", [2, 1024], F32)
    ya_in = nc.dram_tensor("ya_in", [T, 512], BF16)
    ya_out = nc.dram_tensor("ya_out", [2 * T, 512], BF16)
    yg_in = nc.dram_tensor("yg_in", [T, 512], BF16)
    yg_out = nc.dram_tensor("yg_out", [2 * T, 512], BF16)
    if DEBUG:
        dbg_hn = nc.dram_tensor("dbg_hn", [128, 16 * T], BF16, kind="ExternalOutput")
        dbg_y = nc.dram_tensor("dbg_y", [T, 1024], BF16, kind="ExternalOutput")
        dbg_q = nc.dram_tensor("dbg_q", [128, 2 * T], BF16, kind="ExternalOutput")
        dbg_mod = nc.dram_tensor("dbg_mod", [1, 3 * D], F32, kind="ExternalOutput")
        dbg_sm = nc.dram_tensor("dbg_sm", [128, 96], F32, kind="ExternalOutput")
        dbg_xh = nc.dram_tensor("dbg_xh", [128, D], BF16, kind="ExternalOutput")

    def bcast_rows(dh, n):
        return bass.AP(dh, 0, [[0, 128], [1, n]])

    with ExitStack() as top:
        sems = {k: top.enter_context(nc.semaphore("s_" + k)) for k in SEMKEYS}
        P = Prog(sems)
        PID = {}

        def sb(stack, name, shape, dt):
            return stack.enter_context(nc.sbuf_tensor("sb_" + name, shape, dt))

        ps = [top.enter_context(nc.psum_tensor(f"ps{i}", [128, 512], F32)) for i in range(7)]
        psT = top.enter_context(nc.psum_tensor("psT", [128, 8, 128], BF16))

        ident = sb(top, "ident", [128, 128], BF16)
        pm = sb(top, "pm", [128, 128], BF16)
        U = sb(top, "U", [128, 128], F32)
        ind = sb(top, "ind", [128, 2], F32)
        gateB = sb(top, "gateB", [128, D], F32)
        wA = sb(top, "wA", [128, 128], F32)
        wB = sb(top, "wB", [128, 256], F32)
        aT = sb(top, "aT", [128, 16], F32)
        shT = sb(top, "shT", [128, 16], F32)
        nwT = sb(top, "nwT", [128, 16], F32)
        neglam = sb(top, "neglam", [128, 1], F32)
        lamb = sb(top, "lamb", [128, 2, 2, 64], F32)
        lamp = sb(top, "lamp", [128, 2, 64], F32)
        lams = sb(top, "lams", [128, 2], F32)
        lame = sb(top, "lame", [128, 2], F32)
        one11 = sb(top, "one11", [1, 1], F32)
        onesrow = sb(top, "onesrow", [1, 128], F32)
        sm = sb(top, "sm", [128, 64], F32)

        P.dma("pool", lambda e: e.dma_start(out=ident[:], in_=ident_d.ap()), [], ["ident"])
        P.dma("pool", lambda e: e.dma_start(out=pm[:], in_=pm_d.ap()), [], ["pm"])
        P.dma("sp", lambda e: e.dma_start(out=U[:], in_=U_d.ap()), [], ["U"])
        P.dma("sp", lambda e: e.dma_start(out=ind[:], in_=ind_d.ap()), [], ["ind"])
        P.dma("sp", lambda e: e.dma_start(out=nwT[:], in_=nwT_d.ap()), [], ["nwT"])
        P.dma("sp", lambda e: e.dma_start(out=wA[:], in_=bcast_rows(dnw_d, 128)), [], ["wA"])
        P.dma("sp", lambda e: e.dma_start(out=wB[:], in_=bcast_rows(gnw_d, 256)), [], ["wB"])
        P.dma("sp", lambda e: e.dma_start(
            out=lamb[:].rearrange("p a b c -> p (a b c)"), in_=bcast_rows(lam4_d, 256)), [], ["lamb"])
        P.op("dve", lambda e: e.memset(one11[:], 1.0), [], ["one11"])
        P.op("dve", lambda e: e.memset(onesrow[:], 1.0), [], ["onesrow"])
        P.op("dve", lambda e: e.tensor_scalar(out=wA[:], in0=wA[:], scalar1=0.8, scalar2=None,
                                              op0=ALU.mult), ["wA"], ["wA"])
        P.op("dve", lambda e: e.tensor_tensor(
            out=lamp[:], in0=lamb[:, :, 0, :], in1=lamb[:, :, 1, :], op=ALU.mult), ["lamb"], ["lamp"])
        P.op("dve", lambda e: e.tensor_reduce(
            out=lams[:], in_=lamp[:], axis=mybir.AxisListType.X, op=ALU.add), ["lamp"], ["lams"])
        P.op("act", lambda e: e.activation(out=lame[:], in_=lams[:], func=AF.Exp), ["lams"], ["lame"])
        P.op("dve", lambda e: e.tensor_tensor(
            out=neglam[:], in0=lame[:, 1:2], in1=lame[:, 0:1], op=ALU.subtract), ["lame"], ["neglam"])
        P.op("dve", lambda e: e.tensor_scalar(out=neglam[:], in0=neglam[:], scalar1=-0.2, scalar2=None,
                                              op0=ALU.add), ["neglam"], ["neglam"])

        sH = None
        sW = None
        try:
            if STOP == -1:
                raise _Stop()
            sH = ExitStack()
            hnT = sb(sH, "hnT", [128, 16, T], BF16)
            pairs = [[0, 1], [2, 3], [4, 5], [6, 7]]
            sW = ExitStack()
            wbuf0_ = sb(sW, "wbuf0", [128, 16, 512], BF16)
            wglr = sb(sW, "wglr", [128, 16, 16], BF16)
            with ExitStack() as s01:
                cT_sb = sb(s01, "cT_sb", [128, 16], F32)
                cact = sb(s01, "cact", [128, 16], BF16)
                wring = [sb(s01, f"wring{i}", [128, 2048], BF16) for i in range(8)]
                badaB = sb(s01, "badaB", [1, 3072], F32)
                modsb = sb(s01, "modsb", [1, 3072], F32)
                modss = sb(s01, "modss", [1, 4096], F32)
                modg = modss
                xb = [sb(s01, f"xb{i}", [128, D], F32) for i in range(2)]
                junk = sb(s01, "junk", [128, D], BF16)
                xh = sb(s01, "xh", [128, 4, D], BF16)
                P.dma("sp", lambda e: e.dma_start(out=cT_sb[:], in_=cT_d.ap()), [], ["cT"])
                P.dma("sp", lambda e: e.dma_start(out=badaB[:], in_=bada_d.ap()), [], ["badaB"])
                def ss_dma(kk):
                    P.dma("pool", lambda e: e.dma_start(
                        out=wring[kk % 8][:], in_=wada_d.ap()[:, kk * 3072:kk * 3072 + 2048]),
                        [], [f"wring{kk % 8}"])

                def gate_dma(g_):
                    P.dma("pool", lambda e: e.dma_start(
                        out=wring[g_][:].rearrange("p (a c) -> p a c", a=2),
                        in_=wada_d.ap().rearrange("p (k c) -> p k c", k=16)[:, 2 * g_:2 * g_ + 2, 2048:3072]),
                        [], [f"wring{g_}"])

                for kk in range(8):
                    ss_dma(kk)
                P.op("act", lambda e: e.activation(out=cact[:], in_=cT_sb[:], func=AF.Silu), ["cT"], ["cact"])

                def front(t):
                    i = t % 4
                    b_ = xb[t % 2]
                    bn = f"xb{t % 2}"
                    c0 = 2 * (t % 8)
                    P.dma("sp", lambda e: e.dma_start(
                        out=b_[:], in_=x_d.ap()[t * 128:(t + 1) * 128, :]), [], [bn])
                    P.op("act", lambda e: e.activation(
                        out=junk[:], in_=b_[:], func=AF.Square, accum_out=sm[:, c0:c0 + 1]),
                        [bn], ["junk", f"sm{c0}"])
                    P.op("act", lambda e: e.activation(
                        out=sm[:, c0 + 1:c0 + 2], in_=sm[:, c0:c0 + 1], func=AF.Ln, scale=1.0 / D, bias=EPS),
                        [f"sm{c0}"], [f"sm{c0 + 1}"])
                    P.op("act", lambda e: e.activation(
                        out=sm[:, c0:c0 + 1], in_=sm[:, c0 + 1:c0 + 2], func=AF.Exp, scale=-0.5),
                        [f"sm{c0 + 1}"], [f"sm{c0}"])
                    P.op("dve", lambda e: e.tensor_scalar(
                        out=xh[:, i, :], in0=b_[:], scalar1=sm[:, c0:c0 + 1], scalar2=None, op0=ALU.mult),
                        [bn, f"sm{c0}"], [f"xh{i}"])

                for t in range(4):
                    front(t)

                for k in range(16):
                    for cg in range(4):
                        P.op("pe", lambda e, k=k, cg=cg: e.matmul(
                            ps[cg][0:1, 0:512], lhsT=cact[:, k:k + 1],
                            rhs=wring[k % 8][:, cg * 512:(cg + 1) * 512],
                            start=(k == 0), stop=(k == 15)), ["cact", f"wring{k % 8}"], [f"ps{cg}"],
                            milestone=(cg == 3))
                    if k + 8 < 16:
                        ss_dma(k + 8)
                for cg in range(4):
                    P.op("dve", lambda e, cg=cg: e.tensor_tensor(
                        out=modsb[0:1, cg * 512:(cg + 1) * 512], in0=ps[cg][0:1, 0:512],
                        in1=badaB[0:1, cg * 512:(cg + 1) * 512], op=ALU.add), [f"ps{cg}", "badaB"], ["modsb"])
                P.dma("sp", lambda e: e.dma_start(out=mss_in.ap(), in_=modsb[0:1, 0:2048]), ["modsb"], ["mss_in"])
                P.cc(lambda e: e.collective_compute(
                    "AllGather", ALU.bypass, replica_groups=pairs, ins=[mss_in.ap()], outs=[mss_out.ap()]),
                    ["mss_in"], ["mss_out"])
                P.dma("sp", lambda e: e.dma_start(
                    out=modss[0:1, :], in_=bass.AP(mss_out, 0, [[0, 1], [1, 4096]])), ["mss_out"], ["modss"])
                for j in range(32):
                    jj = j % 16
                    off = (jj // 8) * 2048 + (j // 16) * 1024 + (jj % 8) * 128
                    P.op("pe", lambda e, j=j, off=off: e.matmul(
                        ps[0][:, j:j + 1], lhsT=modss[0:1, off:off + 128], rhs=one11[0:1, 0:1],
                        start=True, stop=True), ["modss", "one11"], ["ps0"], milestone=(j == 31))
                P.op("dve", lambda e: e.scalar_tensor_tensor(
                    out=aT[:], in0=ps[0][:, 16:32], scalar=1.0, in1=nwT[:], op0=ALU.add, op1=ALU.mult),
                    ["ps0", "nwT"], ["aT"])
                P.op("dve", lambda e: e.tensor_copy(out=shT[:], in_=ps[0][:, 0:16]), ["ps0"], ["shT"])

                tb = [(psT[:, 0:4, :], "psT")]
                for bi_ in (4, 5, 6):
                    tb.append((ps[bi_][:, 0:256].bitcast(BF16).rearrange("p (a b) -> p a b", a=4), f"ps{bi_}"))
                def gate_front():
                    for k in range(16):
                        for cg in range(2):
                            P.op("pe", lambda e, k=k, cg=cg: e.matmul(
                                ps[cg][0:1, 0:512], lhsT=cact[:, k:k + 1],
                                rhs=wring[k // 2][:, (k % 2) * 1024 + cg * 512:(k % 2) * 1024 + (cg + 1) * 512],
                                start=(k == 0), stop=(k == 15)), ["cact", f"wring{k // 2}"], [f"ps{cg}"],
                                milestone=(cg == 1))
                    for cg in range(2):
                        P.op("dve", lambda e, cg=cg: e.tensor_tensor(
                            out=modsb[0:1, 2048 + cg * 512:2048 + (cg + 1) * 512], in0=ps[cg][0:1, 0:512],
                            in1=badaB[0:1, 2048 + cg * 512:2048 + (cg + 1) * 512], op=ALU.add),
                            [f"ps{cg}", "badaB"], ["modsb"])
                    P.dma("sp", lambda e: e.dma_start(out=mg_in.ap(), in_=modsb[0:1, 2048:3072]), ["modsb"], ["mg_in"])
                    P.cc(lambda e: e.collective_compute(
                        "AllGather", ALU.bypass, replica_groups=pairs, ins=[mg_in.ap()], outs=[mg_out.ap()]),
                        ["mg_in"], ["mg_out"])

                ti = 0
                for g_ in range(8):
                    s0_ = 2 * (g_ % 2)
                    for j in range(16):
                        tv, tn = tb[ti % 4]
                        ti += 1
                        for i in range(2):
                            P.op("pe", lambda e, i=i, j=j, tv=tv, s0_=s0_: e.transpose(
                                out=tv[:, i, :], in_=xh[:, s0_ + i, j * 128:(j + 1) * 128],
                                identity=ident[:]), [f"xh{s0_ + i}", "ident"], [tn], milestone=(i == 1))
                        srcv = tv[:, 0:2, :].rearrange("p a b -> p (a b)")
                        if j % 4 == 0:
                            P.op("act", lambda e, j=j, g_=g_, srcv=srcv: e.activation(
                                out=hnT[:, j, g_ * 256:(g_ + 1) * 256], in_=srcv, func=AF.Identity,
                                scale=aT[:, j:j + 1], bias=shT[:, j:j + 1]),
                                [tn, "aT", "shT"], [f"hnT_{g_}_{j}"])
                        else:
                            P.op("dve", lambda e, j=j, g_=g_, srcv=srcv: e.tensor_scalar(
                                out=hnT[:, j, g_ * 256:(g_ + 1) * 256], in0=srcv, scalar1=aT[:, j:j + 1],
                                scalar2=shT[:, j:j + 1], op0=ALU.mult, op1=ALU.add),
                                [tn, "aT", "shT"], [f"hnT_{g_}_{j}"])
                    if g_ + 2 < 8:
                        front(2 * (g_ + 2))
                        front(2 * (g_ + 2) + 1)
                    if 1 <= g_ <= 4:
                        gate_dma(2 * (g_ - 1))
                        gate_dma(2 * (g_ - 1) + 1)
                    if g_ == 5:
                        for kk in range(4):
                            P.dma("pool", lambda e, kk=kk: e.dma_start(
                                out=wbuf0_[:, kk * 4:(kk + 1) * 4, :].rearrange("p k c -> p (k c)"),
                                in_=win_d.ap()[4, :, kk * 2048:(kk + 1) * 2048]), [], ["wbuf0"])
                        P.dma("pool", lambda e: e.dma_start(
                            out=wglr[:].rearrange("p k c -> p (k c)"), in_=wglr_d.ap()), [], ["wglr"])
                    if g_ == 6:
                        gate_front()

                P.dma("sp", lambda e: e.dma_start(
                    out=modg[0:1, 0:2048], in_=bass.AP(mg_out, 0, [[0, 1], [1, 2048]])), ["mg_out"], ["modg", "modss"])
                for n in range(4):
                    P.op("pe", lambda e, n=n: e.matmul(
                        ps[2 + n % 2][:, 0:512], lhsT=onesrow[0:1, 0:128],
                        rhs=modg[0:1, n * 512:(n + 1) * 512], start=True, stop=True),
                        ["modg", "onesrow"], [f"ps{2 + n % 2}"])
                    P.op("act", lambda e, n=n: e.activation(
                        out=gateB[:, n * 512:(n + 1) * 512], in_=ps[2 + n % 2][:, 0:512], func=AF.Copy),
                        [f"ps{2 + n % 2}"], ["gateB"])
                P.barrier()
            if DEBUG:
                P.dma("sp", lambda e: e.dma_start(
                    out=dbg_hn.ap(), in_=hnT[:].rearrange("p a b -> p (a b)")), ["hnTall"], [])

            if STOP in (1, 11, 12):
                raise _Stop()
            wbuf = [wbuf0_, sb(sW, "wbuf1", [128, 16, 512], BF16)]
            ystg = [sb(sW, f"ystg{i}", [128, 256], BF16) for i in range(4)]
            ycnt = [0]

            SEQ = [4, 5, 6, 0, 1, 2, 3]
            WB = {g: i % 2 for i, g in enumerate(SEQ)}

            def after_group(g):
                i = SEQ.index(g)
                if i + 2 < len(SEQ):
                    load_group(SEQ[i + 2])

            def load_group(g):
                wb = wbuf[WB[g]]
                for kk in range(4):
                    P.dma("pool", lambda e, g=g, kk=kk, wb=wb: e.dma_start(
                        out=wb[:, kk * 4:(kk + 1) * 4, :].rearrange("p k c -> p (k c)"),
                        in_=win_d.ap()[g, :, kk * 2048:(kk + 1) * 2048]), [], [f"wbuf{WB[g]}"])

            ipc = [0]
            ipbanks = [0, 1, 2]
            pend = []

            def inproj_block(g, blk, tg, M=128, w=None, wname=None):
                todo = pend[:]
                del pend[:]
                bi = ipbanks[ipc[0] % len(ipbanks)]
                ipc[0] += 1
                pst = ps[bi]
                for k in range(16):
                    if w is None:
                        lhs = lambda k=k: wbuf[WB[g]][:, k, blk * 128:blk * 128 + M]
                        wn = f"wbuf{WB[g]}"
                    else:
                        lhs = lambda k=k: w[:, k, 0:M]
                        wn = wname
                    P.op("pe", lambda e, k=k, lhs=lhs, pst=pst: e.matmul(
                        pst[0:M, 0:512], lhsT=lhs(), rhs=hnT[:, k, tg * 512:(tg + 1) * 512],
                        start=(k == 0), stop=(k == 15)), [wn, "hnT"], [f"ps{bi}"], milestone=(k == 15))
                for f_ in todo:
                    f_()
                return pst, f"ps{bi}"

            def flush_pend():
                todo = pend[:]
                del pend[:]
                for f_ in todo:
                    f_()

            trc = [0]

            def evac_transposed(pst, pname, dst_fn, dname, tg, func, rawbufs, rprefix):
                ri = trc[0] % 2
                trc[0] += 1
                raw = rawbufs[ri]
                rn = f"{rprefix}{ri}"
                P.op("act", lambda e: e.activation(out=raw[:], in_=pst[:, 0:512], func=func), [pname], [rn])
                slot = ri

                def post():
                    for i in range(4):
                        P.op("pe", lambda e, i=i: e.transpose(
                            out=psT[:, slot * 4 + i, :], in_=raw[:, i * 128:(i + 1) * 128], identity=ident[:]),
                            [rn, "ident"], ["psT"], milestone=(i == 3))
                    P.op("dve", lambda e: e.tensor_copy(out=dst_fn(), in_=psT[:, slot * 4:(slot + 1) * 4, :]),
                         ["psT"], [dname])

                pend.append(post)

            def ywrite(src_fn, sname, t, c0, width):
                dst = ya_in if c0 < 512 else yg_in
                cc0 = c0 % 512
                P.dma("sp", lambda e: e.dma_start(
                    out=dst.ap()[t * 128:(t + 1) * 128, cc0:cc0 + width], in_=src_fn()), [sname], [f"y_in_{t}_{c0}"])

            load_group(5)

            with ExitStack() as sG:
                gqT = sb(sG, "gqT", [128, 2, T], BF16)
                gk = sb(sG, "gk", [128, NT, 256], BF16)
                gv = sb(sG, "gv", [128, NT, 512], BF16)
                gg = sb(sG, "gg", [128, NT, 512], BF16)
                glrT = sb(sG, "glrT", [32, T], F32)
                w2aug = sb(sG, "w2aug", [17, 256], F32)
                dec = sb(sG, "dec", [128, 2, 32], F32)
                raws = [sb(sG, f"graw{i}", [128, 512], BF16) for i in range(2)]
                e1 = [sb(sG, f"e1_{i}", [128, 256], F32) for i in range(2)]
                lap = [sb(sG, f"lap{i}", [128, 256], F32) for i in range(2)]
                ed = [sb(sG, f"ed{i}", [128, 256], F32) for i in range(2)]
                sqg = sb(sG, "sqg", [128, 256], F32)
                Sst = [sb(sG, f"Sst{i}", [128, 256], F32) for i in range(2)]
                Sbf = sb(sG, "Sbf", [128, 32, 256], BF16)
                gw = [sb(sG, f"gw{i}", [128, 256], F32) for i in range(2)]

                P.dma("sp", lambda e: e.dma_start(out=w2aug[:], in_=w2aug_d.ap()), [], ["w2aug"])
                P.op("pool", lambda e: e.memset(glrT[:], 1.0), [], ["glrT"])
                for blk in range(4):
                    e_ = blk // 2
                    for tg in range(4):
                        pst, pn = inproj_block(4, blk, tg)
                        if blk % 2 == 0:
                            P.op("act", lambda e, pst=pst, e_=e_, tg=tg: e.activation(
                                out=gqT[:, e_, tg * 512:(tg + 1) * 512], in_=pst[:, 0:512], func=AF.Copy,
                                scale=float(128 ** -0.5)), [pn], ["gqT"])
                        else:
                            evac_transposed(pst, pn, lambda e_=e_, tg=tg: gk[:, tg * 4:(tg + 1) * 4,
                                                                            e_ * 128:(e_ + 1) * 128],
                                            "gk", tg, AF.Copy, raws, "graw")
                after_group(4)
                for blk in range(4):
                    for tg in range(4):
                        pst, pn = inproj_block(5, blk, tg)
                        evac_transposed(pst, pn, lambda blk=blk, tg=tg: gv[:, tg * 4:(tg + 1) * 4,
                                                                           blk * 128:(blk + 1) * 128],
                                        "gv", tg, AF.Copy, raws, "graw")
                after_group(5)
                for blk in range(4):
                    for tg in range(4):
                        pst, pn = inproj_block(6, blk, tg)
                        evac_transposed(pst, pn, lambda blk=blk, tg=tg: gg[:, tg * 4:(tg + 1) * 4,
                                                                           blk * 128:(blk + 1) * 128],
                                        "gg", tg, AF.Silu, raws, "graw")
                after_group(6)
                for tg in range(4):
                    pst, pn = inproj_block(None, 0, tg, M=16, w=wglr, wname="wglr")
                    P.op("act", lambda e, pst=pst, tg=tg: e.activation(
                        out=glrT[0:16, tg * 512:(tg + 1) * 512], in_=pst[0:16, 0:512], func=AF.Copy),
                        [pn], ["glrT"])
                flush_pend()
                for t_ in range(NT):
                    for e_ in range(2):
                        P.op("dve", lambda e, t_=t_, e_=e_: e.tensor_tensor(
                            out=gg[:, t_, e_ * 256:(e_ + 1) * 256], in0=gg[:, t_, e_ * 256:(e_ + 1) * 256],
                            in1=wB[:], op=ALU.mult), ["gg", "wB"], ["gg"])
                def prep_a(t):
                    zb = 3 if t % 2 == 0 else 6
                    e1_, lap_ = e1[t % 2], lap[t % 2]
                    P.op("pe", lambda e: e.matmul(
                        ps[zb][:, 0:256], lhsT=glrT[0:17, t * 128:(t + 1) * 128], rhs=w2aug[0:17, :],
                        start=True, stop=True), ["glrT", "w2aug"], [f"ps{zb}"])
                    P.op("act", lambda e: e.activation(out=e1_[:], in_=ps[zb][:, 0:256], func=AF.Exp, scale=-1.0),
                         [f"ps{zb}"], [f"e1_{t % 2}"])
                    P.op("act", lambda e: e.activation(out=lap_[:], in_=e1_[:], func=AF.Ln, bias=1.0),
                         [f"e1_{t % 2}"], [f"lap{t % 2}"])

                def prep_b(t):
                    lap_, ed_ = lap[t % 2], ed[t % 2]
                    P.op("pe", lambda e: e.matmul(ps[4][:, 0:256], lhsT=U[:], rhs=lap_[:], start=True, stop=True),
                         ["U", f"lap{t % 2}"], ["ps4"])
                    for e_ in range(2):
                        P.op("pe", lambda e, e_=e_: e.matmul(
                            ps[5][:, e_ * 2:(e_ + 1) * 2], lhsT=lap_[:, e_ * 128:(e_ + 1) * 128], rhs=ind[:],
                            start=True, stop=True), [f"lap{t % 2}", "ind"], ["ps5"], milestone=(e_ == 1))
                    P.op("act", lambda e: e.activation(out=ed_[:], in_=ps[4][:, 0:256], func=AF.Exp,
                                                       scale=-1.0 / 16.0), ["ps4"], [f"ed{t % 2}"])
                    P.op("act", lambda e: e.activation(
                        out=dec[:, :, 2 * t:2 * t + 2],
                        in_=ps[5][:, 0:4].rearrange("p (a b) -> p a b", a=2), func=AF.Exp, scale=-1.0 / 16.0),
                        ["ps5"], ["dec"])
                    P.op("dve", lambda e: e.tensor_tensor(
                        out=gk[:, t, :], in0=gk[:, t, :], in1=ed_[:], op=ALU.mult), ["gk", f"ed{t % 2}"], ["gk"])

                prep_a(0)
                for t in range(NT):
                    if t + 1 < NT:
                        prep_a(t + 1)
                    prep_b(t)

                def scan_step(e_, c):
                    t, h = c // 2, c % 2
                    bi = 3 + c % 2
                    P.op("pe", lambda e: e.matmul(
                        ps[bi][:, 0:256], lhsT=gk[64 * h:64 * h + 64, t, e_ * 128:(e_ + 1) * 128],
                        rhs=gv[64 * h:64 * h + 64, t, e_ * 256:(e_ + 1) * 256], start=True, stop=True),
                        ["gk", "gv"], [f"ps{bi}"])
                    sn, so = Sst[(c + 1) % 2], Sst[c % 2]
                    if c == 0:
                        P.op("dve", lambda e: e.tensor_copy(out=sn[:], in_=ps[bi][:, 0:256]),
                             [f"ps{bi}"], [f"Sst{(c + 1) % 2}"])
                    else:
                        P.op("dve", lambda e: e.scalar_tensor_tensor(
                            out=sn[:], in0=so[:], scalar=dec[:, e_, c:c + 1], in1=ps[bi][:, 0:256],
                            op0=ALU.mult, op1=ALU.add),
                            [f"ps{bi}", f"Sst{c % 2}", "dec"], [f"Sst{(c + 1) % 2}"])
                    P.op("dve", lambda e: e.tensor_copy(out=Sbf[:, c, :], in_=sn[:]),
                         [f"Sst{(c + 1) % 2}"], [f"Sbf{c}"])

                def out_tile(e_, t):
                    bi = 5 + t % 2
                    for h in range(2):
                        c = 2 * t + h
                        P.op("pe", lambda e, c=c, h=h: e.matmul(
                            ps[bi][64 * h:64 * h + 64, 0:256], lhsT=gqT[:, e_, c * 64:(c + 1) * 64],
                            rhs=Sbf[:, c, :], start=True, stop=True, tile_position=(0, 64 * h)),
                            ["gqT", f"Sbf{c}"], [f"ps{bi}"], milestone=(h == 1))
                    c0 = 32 + 2 * (t % 8)
                    g_ = gw[t % 2]
                    ys = ystg[ycnt[0] % 4]
                    ysn = f"ystg{ycnt[0] % 4}"
                    ycnt[0] += 1
                    P.op("act", lambda e: e.activation(
                        out=sqg[:], in_=ps[bi][:, 0:256], func=AF.Square, accum_out=sm[:, c0:c0 + 1]),
                        [f"ps{bi}"], ["sqg", f"sm{c0}"])
                    P.op("act", lambda e: e.activation(
                        out=sm[:, c0 + 1:c0 + 2], in_=sm[:, c0:c0 + 1], func=AF.Ln, scale=1.0 / 256, bias=EPS),
                        [f"sm{c0}"], [f"sm{c0 + 1}"])
                    P.op("act", lambda e: e.activation(
                        out=sm[:, c0:c0 + 1], in_=sm[:, c0 + 1:c0 + 2], func=AF.Exp, scale=-0.5),
                        [f"sm{c0 + 1}"], [f"sm{c0}"])
                    P.op("dve", lambda e: e.scalar_tensor_tensor(
                        out=ys[:], in0=ps[bi][:, 0:256], scalar=sm[:, c0:c0 + 1],
                        in1=gg[:, t, e_ * 256:(e_ + 1) * 256], op0=ALU.mult, op1=ALU.mult),
                        [f"ps{bi}", f"sm{c0}", "gg"], [ysn])
                    ywrite(lambda: ys[:], ysn, t, 512 + e_ * 256, 256)

                for c in range(32):
                    scan_step(0, c)
                for t in range(NT):
                    out_tile(0, t)
                    scan_step(1, 2 * t)
                    scan_step(1, 2 * t + 1)
                for t in range(NT):
                    out_tile(1, t)
                P.barrier()

            pairs = [[0, 1], [2, 3], [4, 5], [6, 7]]
            P.cc(lambda e: e.collective_compute(
                "AllGather", ALU.bypass, replica_groups=pairs, ins=[yg_in.ap()], outs=[yg_out.ap()]),
                [], ["yg_out"])
            if STOP == 2:
                raise _Stop()
            with ExitStack() as sA:
                cosT = sb(sA, "cosT", [128, T], F32)
                sinT = sb(sA, "sinT", [128, T], F32)
                qT2 = [sb(sA, f"qT{i}", [128, T], BF16) for i in range(2)]
                kT2 = [[sb(sA, f"kT{i}_{c}", [128, T], BF16) for c in range(2)] for i in range(2)]
                Vp2 = [sb(sA, f"Vp{i}", [128, NT, 130], BF16) for i in range(2)]
                Gs2 = [sb(sA, f"Gs{i}", [128, NT, 128], BF16) for i in range(2)]
                raws = [sb(sA, f"araw{i}", [128, 512], BF16) for i in range(2)]
                rraw = [sb(sA, f"rraw{i}", [128, 512], BF16) for i in range(2)]
                t1 = [sb(sA, f"t1_{i}", [128, 512], F32) for i in range(2)]
                t2 = [sb(sA, f"t2_{i}", [128, 512], F32) for i in range(2)]
                NPT = 12
                pT = [sb(sA, f"pT{i}", [128, 2, 2, 128], BF16) for i in range(NPT)]
                o1 = [sb(sA, f"o1_{i}", [128, 128], F32) for i in range(6)]
                o2 = [sb(sA, f"o2_{i}", [128, 128], F32) for i in range(6)]
                gwa = [sb(sA, f"gwa{i}", [128, 128], F32) for i in range(6)]
                smE = sb(sA, "smE", [128, 48], F32)
                epc = [0]
                sqj = sb(sA, "sqj", [128, 128], F32)
                psTf = psT[:].rearrange("p a b -> p (a b)").bitcast(F32)
                P.dma("sp", lambda e: e.dma_start(out=cosT[:], in_=cos_d.ap()), [], ["cosT"])
                P.dma("sp", lambda e: e.dma_start(out=sinT[:], in_=sin_d.ap()), [], ["sinT"])
                for b_ in range(2):
                    P.op("pool", lambda e, b_=b_: e.memset(Vp2[b_][:], 1.0), [], [f"Vp{b_}"])
                    P.op("pool", lambda e, b_=b_: e.memset(kT2[b_][0][64:128, :], 0.0), [], [f"kT{b_}"])
                    P.op("pool", lambda e, b_=b_: e.memset(kT2[b_][1][0:64, :], 0.0), [], [f"kT{b_}"])
                rc = [0]
                pc = [0]
                sc = [0]
                del ipbanks[:]
                ipbanks.extend([0, 1])

                def inproj_closures(hd):
                    bb = hd % 2
                    out = []
                    for blk in range(4):
                        for tg in range(4):
                            def blockfn(blk=blk, tg=tg):
                                todo = pend[:]
                                del pend[:]
                                bi = ipbanks[ipc[0] % len(ipbanks)]
                                ipc[0] += 1
                                pst, pn = ps[bi], f"ps{bi}"
                                for k in range(16):
                                    P.op("pe", lambda e, k=k: e.matmul(
                                        pst[:, 0:512], lhsT=wbuf[WB[hd]][:, k, blk * 128:(blk + 1) * 128],
                                        rhs=hnT[:, k, tg * 512:(tg + 1) * 512], start=(k == 0), stop=(k == 15)),
                                        [f"wbuf{WB[hd]}", "hnT"], [pn], milestone=(k == 15))
                                    if k % 4 == 3 and k < 15:
                                        yield
                                for f_ in todo:
                                    f_()
                                if blk < 2:
                                    ri = rc[0] % 2
                                    rc[0] += 1
                                    rr_, a1, a2 = rraw[ri], t1[ri], t2[ri]
                                    P.op("act", lambda e: e.activation(
                                        out=rr_[:], in_=pst[:, 0:512], func=AF.Copy), [pn], [f"rraw{ri}"])

                                    def rpost():
                                        P.op("pe", lambda e: e.matmul(
                                            psTf, lhsT=pm[:], rhs=rr_[:], start=True, stop=True),
                                            ["pm", f"rraw{ri}"], ["psT"])
                                        P.op("dve", lambda e: e.tensor_tensor(
                                            out=a1[:], in0=rr_[:], in1=cosT[:, tg * 512:(tg + 1) * 512],
                                            op=ALU.mult), [f"rraw{ri}", "cosT"], [f"t1_{ri}"])
                                        P.op("dve", lambda e: e.tensor_tensor(
                                            out=a2[:], in0=psTf, in1=sinT[:, tg * 512:(tg + 1) * 512],
                                            op=ALU.mult), ["psT", "sinT"], [f"t2_{ri}"])
                                        if blk == 0:
                                            P.op("dve", lambda e: e.tensor_tensor(
                                                out=qT2[bb][:, tg * 512:(tg + 1) * 512], in0=a1[:], in1=a2[:],
                                                op=ALU.add), [f"t1_{ri}", f"t2_{ri}"], [f"qT{bb}"])
                                        else:
                                            for c in range(2):
                                                P.op("dve", lambda e, c=c: e.tensor_tensor(
                                                    out=kT2[bb][c][64 * c:64 * c + 64, tg * 512:(tg + 1) * 512],
                                                    in0=a1[64 * c:64 * c + 64, :], in1=a2[64 * c:64 * c + 64, :],
                                                    op=ALU.add), [f"t1_{ri}", f"t2_{ri}"], [f"kT{bb}"])

                                    pend.append(rpost)
                                elif blk == 2:
                                    evac_transposed(pst, pn, lambda: Vp2[bb][:, tg * 4:(tg + 1) * 4, 0:128],
                                                    f"Vp{bb}", tg, AF.Copy, raws, "araw")
                                else:
                                    evac_transposed(pst, pn, lambda: Gs2[bb][:, tg * 4:(tg + 1) * 4, :],
                                                    f"Gs{bb}", tg, AF.Silu, raws, "araw")
                            out.append(blockfn)
                    return out

                for g_ in inproj_closures(0):
                    for _ in g_():
                        pass
                after_group(0)
                for hd in range(4):
                    bb = hd % 2
                    qT, kTc, Vp, Gs = qT2[bb], kT2[bb], Vp2[bb], Gs2[bb]
                    flush_pend()
                    fill = inproj_closures(hd + 1) if hd < 3 else []
                    steps = [(qi, [k_ for k_ in (kp, kp + 1) if k_ <= qi])
                             for qi in range(NT) for kp in range(0, qi + 1, 2)]
                    burst = {}

                    def acc_of(qi):
                        bank = 5 + qi % 2
                        return [ps[bank][:, 0:129], ps[bank][:, 256:385]], f"ps{bank}"

                    def emit_qk(qi, kis, qT=qT, kTc=kTc, bb=bb):
                        sbi = 2 + sc[0] % 3
                        sc[0] += 1
                        psS = ps[sbi][:, 0:512].rearrange("p (j c q) -> p j c q", j=2, c=2)
                        for j, ki in enumerate(kis):
                            for c in range(2):
                                P.op("pe", lambda e, j=j, ki=ki, c=c: e.matmul(
                                    psS[:, j, c, :], lhsT=kTc[c][:, ki * 128:(ki + 1) * 128],
                                    rhs=qT[:, qi * 128:(qi + 1) * 128], start=True, stop=True),
                                    [f"kT{bb}", f"qT{bb}"], [f"ps{sbi}"],
                                    milestone=(j == len(kis) - 1 and c == 1))
                        pi = pc[0] % NPT
                        pc[0] += 1
                        pt = pT[pi]
                        nk = len(kis)
                        P.op("act", lambda e: e.activation(
                            out=pt[:, 0:nk], in_=psS[:, 0:nk], func=AF.Exp, scale=0.125),
                            [f"ps{sbi}"], [f"pT{pi}"])
                        if qi in kis:
                            j = kis.index(qi)
                            P.op("dve", lambda e, j=j: e.memset(pt[64:128, j, :, 0:64], 0.0), [], [f"pT{pi}"])
                        return (qi, kis, pt, pi)

                    def emit_av(qi, kis, pt, pi, Vp=Vp, bb=bb):
                        accs, accn = acc_of(qi)
                        for j, ki in enumerate(kis):
                            P.op("pe", lambda e, j=j, ki=ki: e.matmul(
                                accs[0], lhsT=pt[:, j, 0, :], rhs=Vp[:, ki, 0:129],
                                start=(ki == 0), stop=(ki == qi)), [f"pT{pi}", f"Vp{bb}"], [accn],
                                milestone=(j == len(kis) - 1))
                            burst.setdefault(qi, []).append((pt, pi, j, ki))
                        if kis[-1] == qi:
                            items = burst.pop(qi)
                            for n_, (pt_, pi_, j_, ki_) in enumerate(items):
                                P.op("pe", lambda e, pt_=pt_, j_=j_, ki_=ki_, n_=n_: e.matmul(
                                    accs[1], lhsT=pt_[:, j_, 1, :], rhs=Vp[:, ki_, 0:129],
                                    start=(n_ == 0), stop=(n_ == len(items) - 1)), [f"pT{pi_}", f"Vp{bb}"],
                                    [accn], milestone=(n_ == len(items) - 1))
                            emit_epilogue(qi)

                    def emit_epilogue(qi, hd=hd, Gs=Gs, bb=bb):
                        accs, accn = acc_of(qi)
                        r6 = epc[0] % 6
                        epc[0] += 1
                        oa, ob, g_ = o1[r6], o2[r6], gwa[r6]
                        smn = f"smE{r6}"
                        b0 = 8 * r6
                        for c in range(2):
                            P.op("dve", lambda e, c=c: e.reciprocal(
                                out=smE[:, b0 + c:b0 + c + 1], in_=accs[c][:, 128:129]), [accn], [smn])
                        P.op("dve", lambda e: e.tensor_tensor(
                            out=smE[:, b0 + 2:b0 + 3], in0=smE[:, b0 + 1:b0 + 2], in1=neglam[:], op=ALU.mult),
                            [smn, "neglam"], [smn])
                        P.op("dve", lambda e: e.tensor_scalar(
                            out=oa[:], in0=accs[0][:, 0:128], scalar1=smE[:, b0:b0 + 1], scalar2=None,
                            op0=ALU.mult), [accn, smn], [f"o1_{r6}"])
                        P.op("dve", lambda e: e.scalar_tensor_tensor(
                            out=ob[:], in0=accs[1][:, 0:128], scalar=smE[:, b0 + 2:b0 + 3], in1=oa[:],
                            op0=ALU.mult, op1=ALU.add), [accn, smn, f"o1_{r6}"], [f"o2_{r6}"])
                        P.op("dve", lambda e: e.tensor_tensor(
                            out=g_[:], in0=Gs[:, qi, :], in1=wA[:], op=ALU.mult), [f"Gs{bb}", "wA"], [f"gwa{r6}"])
                        P.op("dve", lambda e: e.scalar_tensor_tensor(
                            out=sqj[:], in0=ob[:], scalar=1.0, in1=ob[:], op0=ALU.mult, op1=ALU.mult,
                            accum_out=smE[:, b0 + 3:b0 + 4]), [f"o2_{r6}"], ["sqj", smn + "s"])

                        def stage_b():
                            P.op("act", lambda e: e.activation(
                                out=smE[:, b0 + 4:b0 + 5], in_=smE[:, b0 + 3:b0 + 4], func=AF.Ln,
                                scale=1.0 / 128, bias=EPS), [smn + "s"], [smn + "t"])
                            P.op("act", lambda e: e.activation(
                                out=smE[:, b0 + 5:b0 + 6], in_=smE[:, b0 + 4:b0 + 5], func=AF.Exp, scale=-0.5),
                                [smn + "t"], [smn + "u"])
                            deferred.append([2, stage_c])

                        def stage_c():
                            ys = ystg[ycnt[0] % 4]
                            ysn = f"ystg{ycnt[0] % 4}"
                            ycnt[0] += 1
                            P.op("dve", lambda e: e.scalar_tensor_tensor(
                                out=ys[:, 0:128], in0=ob[:], scalar=smE[:, b0 + 5:b0 + 6], in1=g_[:],
                                op0=ALU.mult, op1=ALU.mult), [f"o2_{r6}", smn + "u", f"gwa{r6}"], [ysn])
                            ywrite(lambda: ys[:, 0:128], ysn, qi, hd * 128, 128)

                        deferred.append([4, stage_b])

                    def tick():
                        for d_ in deferred:
                            d_[0] -= 1
                        while deferred and deferred[0][0] <= 0:
                            deferred.pop(0)[1]()

                    curgen = [None]

                    def advance_fill():
                        if curgen[0] is None:
                            if not fill:
                                return
                            curgen[0] = fill.pop(0)()
                        try:
                            next(curgen[0])
                        except StopIteration:
                            curgen[0] = None

                    deferred = []
                    inflight = []
                    for si, (qi, kis) in enumerate(steps):
                        tick()
                        inflight.append(emit_qk(qi, kis))
                        if len(inflight) > 2:
                            emit_av(*inflight.pop(0))
                        for _ in range(FILL_BURST if si % FILL_EVERY == 1 else 0):
                            advance_fill()
                    while inflight:
                        tick()
                        emit_av(*inflight.pop(0))
                    while deferred:
                        deferred.pop(0)[1]()
                    while fill or curgen[0] is not None:
                        advance_fill()
                    if hd < 3:
                        after_group(hd + 1)
                    if hd == 2 and STOP > 37:
                        flush_pend()
                        for kk in range(8):
                            P.dma("pool", lambda e, kk=kk: e.dma_start(
                                out=hnT[:, kk * 2:(kk + 1) * 2, :].rearrange("p k c -> p (k c)"),
                                in_=wout_d.ap()[:, kk * 4096:(kk + 1) * 4096]),
                                ([] if kk == 0 else ["hnT"]), (["hnT", "wout0"] if kk == 0 else [f"wout{kk}"]))
                P.barrier()
            sW.close()
            sW = None

            if STOP in (3, 31, 32, 33, 34, 35, 36, 37):
                raise _Stop()
            P.cc(lambda e: e.collective_compute(
                "AllGather", ALU.bypass, replica_groups=pairs, ins=[ya_in.ap()], outs=[ya_out.ap()]),
                [], ["ya_out"])
            if DEBUG:
                P.dma("sp", lambda e: e.dma_start(out=dbg_y.ap()[:, 0:512], in_=ya_in.ap()), [], [])
                P.dma("sp", lambda e: e.dma_start(out=dbg_y.ap()[:, 512:1024], in_=yg_in.ap()), [], [])

            if STOP == 4:
                raise _Stop()
            with ExitStack() as sO:
                wout = hnT
                fnwB = sb(sO, "fnwB", [128, D], F32)
                Yt = [sb(sO, f"Yt{i}", [128, 1024], BF16) for i in range(3)]
                yT = [sb(sO, f"yT{i}", [128, 8, 128], BF16) for i in range(2)]
                xo = [sb(sO, f"xo{i}", [128, D], F32) for i in range(3)]
                hb = [sb(sO, f"hb{i}", [128, D], F32) for i in range(8)]
                tmpm = [sb(sO, f"tmpm{i}", [128, 512], F32) for i in range(2)]
                junk2 = sb(sO, "junk2", [128, D], BF16)
                P.dma("sp", lambda e: e.dma_start(out=fnwB[:], in_=bcast_rows(fnw_d, D)), [], ["fnwB"])
                ycn = [0]

                def rank_of(e):
                    if "rank" not in PID:
                        PID["rank"] = e.partition_id() % 2
                    return PID["rank"]

                def loads(i, other):
                    slot = ycn[0] % 3
                    ycn[0] += 1
                    Y = Yt[slot]
                    for piece, (loc_t, gat_t, cname) in enumerate(((ya_in, ya_out, "ya_out"),
                                                                   (yg_in, yg_out, "yg_out"))):
                        def ld(e, i=i, Y=Y, piece=piece, loc_t=loc_t, gat_t=gat_t):
                            rank = rank_of(e)
                            if other:
                                src = gat_t.ap().rearrange("(q i p) c -> q i p c", q=4, i=8)
                                sel = src[ds(2 - rank, 1), i, :, :]
                            else:
                                src = loc_t.ap().rearrange("(h i p) c -> h i p c", h=2, i=8)
                                sel = src[ds(rank, 1), i, :, :]
                            return e.dma_start(out=Y[:, piece * 512:(piece + 1) * 512],
                                               in_=sel.rearrange("a p c -> (a p) c"))
                        P.dma("sp", ld, ([cname] if other else []), [f"Yt{slot}_{piece}"])
                    return slot

                def tchunk(slot, tslot, n):
                    Y, yt_ = Yt[slot], yT[tslot]
                    for a_ in range(4):
                        k = 4 * n + a_
                        P.op("pe", lambda e, a_=a_, k=k, Y=Y: e.transpose(
                            out=psT[:, a_, :], in_=Y[:, k * 128:(k + 1) * 128], identity=ident[:]),
                            [f"Yt{slot}_{k // 4}", "ident"], ["psT"], milestone=(a_ == 3))
                    P.op("act", lambda e, n=n, yt_=yt_: e.activation(
                        out=yt_[:, 4 * n:4 * n + 4, :], in_=psT[:, 0:4, :], func=AF.Copy),
                        ["psT"], [f"yT{tslot}"])

                def xload(i):
                    P.dma("sp", lambda e: e.dma_start(
                        out=xo[i % 3][:], in_=xo_d.ap()[i * 128:(i + 1) * 128, :]), [], [f"xo{i % 3}"])

                def scale_chunk(k):
                    P.op("dve", lambda e: e.tensor_tensor(
                        out=wout[:, k, :], in0=wout[:, k, :], in1=gateB[:], op=ALU.mult),
                        [f"wout{k // 2}", "gateB"], [f"wout{k // 2}"])

                for k in range(8):
                    scale_chunk(k)
                for phase in range(2):
                    other = phase == 1
                    if other:
                        for k in range(8, 16):
                            scale_chunk(k)
                    slots = {}
                    slots[0] = loads(0, other)
                    slots[1] = loads(1, other)
                    if not other:
                        xload(0)
                        xload(1)
                    for n in range(2):
                        tchunk(slots[0], 0, n)
                    for i in range(8):
                        yt_, hb_ = yT[i % 2], hb[i]
                        for n in range(4):
                            bi = n
                            for k in range(8):
                                kk = 8 * phase + k
                                P.op("pe", lambda e, k=k, kk=kk, n=n, bi=bi, yt_=yt_: e.matmul(
                                    ps[bi][:, 0:512], lhsT=yt_[:, k, :], rhs=wout[:, kk, n * 512:(n + 1) * 512],
                                    start=(k == 0), stop=(k == 7)), [f"yT{i % 2}", f"wout{kk // 2}"],
                                    [f"ps{bi}"], milestone=(k == 7))
                            if i + 1 < 8 and n < 2:
                                tchunk(slots[i + 1], (i + 1) % 2, n)
                            if not other:
                                xo_ = xo[i % 3]
                                P.op("dve", lambda e, bi=bi, n=n, xo_=xo_, hb_=hb_: e.tensor_tensor(
                                    out=hb_[:, n * 512:(n + 1) * 512], in0=ps[bi][:, 0:512],
                                    in1=xo_[:, n * 512:(n + 1) * 512], op=ALU.add),
                                    [f"ps{bi}", f"xo{i % 3}"], [f"hb{i}_{n}"])
                            else:
                                P.op("dve", lambda e, bi=bi, n=n, hb_=hb_: e.tensor_tensor(
                                    out=hb_[:, n * 512:(n + 1) * 512], in0=ps[bi][:, 0:512],
                                    in1=hb_[:, n * 512:(n + 1) * 512], op=ALU.add),
                                    [f"ps{bi}", f"hb{i}_{n}"], [f"hb{i}_{n}"])
                        if i + 2 < 8:
                            slots[i + 2] = loads(i + 2, other)
                            if not other:
                                xload(i + 2)
                        if other:
                            c0 = 2 * i
                            ob_ = xo[i % 3]
                            hn4 = [f"hb{i}_{n}" for n in range(4)]
                            P.op("act", lambda e, hb_=hb_, c0=c0: e.activation(
                                out=junk2[:], in_=hb_[:], func=AF.Square, accum_out=sm[:, c0:c0 + 1]),
                                hn4, ["junk2", f"sm{c0}"])
                            P.op("act", lambda e, c0=c0: e.activation(
                                out=sm[:, c0 + 1:c0 + 2], in_=sm[:, c0:c0 + 1], func=AF.Ln, scale=1.0 / D,
                                bias=EPS), [f"sm{c0}"], [f"sm{c0 + 1}"])
                            P.op("act", lambda e, c0=c0: e.activation(
                                out=sm[:, c0:c0 + 1], in_=sm[:, c0 + 1:c0 + 2], func=AF.Exp, scale=-0.5),
                                [f"sm{c0 + 1}"], [f"sm{c0}"])
                            P.op("dve", lambda e, hb_=hb_, ob_=ob_, c0=c0: e.scalar_tensor_tensor(
                                out=ob_[:], in0=hb_[:], scalar=sm[:, c0:c0 + 1], in1=fnwB[:], op0=ALU.mult,
                                op1=ALU.mult), hn4 + [f"sm{c0}", "fnwB"], [f"xo{i % 3}"])
                            P.dma("sp", lambda e, i=i, ob_=ob_: e.dma_start(
                                out=out_d.ap()[i * 128:(i + 1) * 128, :], in_=ob_[:]), [f"xo{i % 3}"],
                                ["out%d" % i])
                P.barrier()
            sH.close()
            sH = None
        except _Stop:
            P.barrier()
            for st_ in (sW, sH):
                if st_ is not None:
                    st_.close()

        with nc.Block() as block:
            @block.sync
            def _(e):
                for f in P.q["sp"]:
                    f(e)

            @block.scalar
            def _(e):
                for f in P.q["act"]:
                    f(e)

            @block.vector
            def _(e):
                for f in P.q["dve"]:
                    f(e)

            @block.gpsimd
            def _(e):
                for f in P.q["pool"]:
                    f(e)

            @block.tensor
            def _(e):
                for f in P.q["pe"]:
                    f(e)
    return nc


def _chunked(w):
    C = w.shape[1]
    return np.ascontiguousarray(w.reshape(16, 128, C).transpose(1, 0, 2).reshape(128, 16 * C))


def _consts():
    ident = np.eye(128, dtype=np.float32)
    pm = np.zeros((128, 128), np.float32)
    for m in range(128):
        partner = m + 32 if (m % 64) < 32 else m - 32
        pm[partner, m] = 1.0
    U = np.zeros((128, 128), np.float32)
    for j in range(128):
        for t in range(128):
            if j > t and j // 64 == t // 64:
                U[j, t] = 1.0
    ind = np.zeros((128, 2), np.float32)
    ind[:64, 0] = 1.0
    ind[64:, 1] = 1.0
    pos = np.arange(T, dtype=np.float32)
    inv_freq = (np.float32(10000.0) ** (-np.arange(0, 64, 2, dtype=np.float32) / np.float32(64))).astype(np.float32)
    ang = (pos[None, :] * inv_freq[:, None]).astype(np.float32)
    cos32 = np.cos(ang).astype(np.float32)
    sin32 = np.sin(ang).astype(np.float32)
    cosT = np.zeros((128, T), np.float32)
    sinT = np.zeros((128, T), np.float32)
    for p in range(128):
        cosT[p] = cos32[p % 32]
        sinT[p] = -sin32[p % 32] if (p % 64) < 32 else sin32[p % 32]
    return dict(ident=ident, pm=pm, U=U, ind=ind, cosT=cosT, sinT=sinT)


def _prepare_in_maps(x, c, norm_w, w_ada, b_ada, w_in, lambda_q1, lambda_k1, lambda_q2, lambda_k2,
                     diff_norm_w, gla_gate_w2, gla_gate_b, gla_norm_w, w_out, final_norm_w):
    f = lambda a: np.asarray(a, dtype=np.float32)
    x, c = f(x), f(c)
    w_ada, b_ada, w_in, w_out = f(w_ada)[0], f(b_ada)[0], f(w_in)[0], f(w_out)[0]
    consts = _consts()
    nwT = np.ascontiguousarray(f(norm_w)[0].reshape(16, 128).T)
    lam4 = np.stack([f(lambda_q1)[0], f(lambda_k1)[0], f(lambda_q2)[0], f(lambda_k2)[0]], 0)
    w2 = f(gla_gate_w2)[0]
    gb = f(gla_gate_b)[0]
    O = [0, 1024, 2048, 3072, 4096, 4608, 5120, 6144, 7168]
    in_maps = []
    for core in range(8):
        b, g = core // 2, core % 2
        groups = []
        for hd in range(4):
            H = 4 * g + hd
            cols = [w_in[:, O[i] + H * 128: O[i] + (H + 1) * 128] for i in range(4)]
            groups.append(np.concatenate(cols, 1))
        e0, e1 = 2 * g, 2 * g + 1
        groups.append(np.concatenate([w_in[:, O[4] + e0 * 128:O[4] + (e0 + 1) * 128],
                                      w_in[:, O[5] + e0 * 128:O[5] + (e0 + 1) * 128],
                                      w_in[:, O[4] + e1 * 128:O[4] + (e1 + 1) * 128],
                                      w_in[:, O[5] + e1 * 128:O[5] + (e1 + 1) * 128]], 1))
        groups.append(w_in[:, O[6] + e0 * 256:O[6] + (e1 + 1) * 256])
        groups.append(w_in[:, O[7] + e0 * 256:O[7] + (e1 + 1) * 256])
        win = np.stack([_chunked(gr) for gr in groups], 0)
        wglr = _chunked(w_in[:, O[8]:O[8] + 16])
        w2aug = np.concatenate([w2[:, e0 * 128:(e1 + 1) * 128], gb[None, e0 * 128:(e1 + 1) * 128]], 0)
        o_ = 1 - g
        perm = np.concatenate([np.arange(g * 512, (g + 1) * 512), np.arange(1024 + g * 512, 1024 + (g + 1) * 512),
                               np.arange(o_ * 512, (o_ + 1) * 512),
                               np.arange(1024 + o_ * 512, 1024 + (o_ + 1) * 512)])
        wout = _chunked(w_out[perm, :])
        m = dict(
            x=np.ascontiguousarray(x[b]),
            xo=np.ascontiguousarray(x[b, g * 1024:(g + 1) * 1024]),
            cT=np.ascontiguousarray(c[b].reshape(16, 128).T),
            wada=_chunked(np.concatenate([w_ada[:, j * 2048 + g * 1024:j * 2048 + (g + 1) * 1024]
                                          for j in range(3)], 1)),
            bada=np.ascontiguousarray(np.concatenate(
                [b_ada[j * 2048 + g * 1024:j * 2048 + (g + 1) * 1024] for j in range(3)])[None, :]),
            nwT=nwT, win=win, wglr=wglr, w2aug=np.ascontiguousarray(w2aug), lam4=np.ascontiguousarray(lam4),
            dnw=f(diff_norm_w)[0][None, :].copy(), gnw=f(gla_norm_w)[0][None, :].copy(),
            fnw=f(final_norm_w)[None, :].copy(), wout=wout,
        )
        m.update(consts)
        in_maps.append(m)
    return in_maps


_NC_CACHE = {}


def kernel(**inputs):
    in_maps = _prepare_in_maps(**inputs)
    if "nc" not in _NC_CACHE:
        _NC_CACHE["nc"] = build_program()
    nc = _NC_CACHE["nc"]
    res = run_bass_kernel_spmd(nc, in_maps, core_ids=list(range(8)))
    out = np.zeros((4, T, D), np.float32)
    for core in range(8):
        b, g = core // 2, core % 2
        out[b, g * 1024:(g + 1) * 1024] = np.asarray(res.results[core]["out"], dtype=np.float32)
    if DEBUG:
        kernel.last = res
    return out
```

```python
import os
from contextlib import ExitStack

import numpy as np
import concourse.bass as bass
import concourse.mybir as mybir
from concourse.bass_utils import run_bass_kernel_spmd

F32 = mybir.dt.float32
BF16 = mybir.dt.bfloat16
AF = mybir.ActivationFunctionType
ALU = mybir.AluOpType
ds = bass.ds

T = 2048
D = 2048
NT = 16
EPS = 1e-6
ENGS = ["pe", "act", "dve", "pool", "sp"]
NSLOT = {"sp": 24, "pool": 12}
SEMKEYS = (["pe", "act", "dve", "pool", "cc0", "cc1", "cc2", "cc3"]
           + [f"dq_{q}{i}" for q, n in NSLOT.items() for i in range(n)])
DEBUG = bool(int(os.environ.get("KDEBUG", "0")))
STOP = int(os.environ.get("KSTOP", "99"))
KEV = os.environ.get("KEV", "")
FILL_BURST = int(os.environ.get("KFB", "2"))
FILL_EVERY = int(os.environ.get("KFE", "2"))


class _Stop(Exception):
    pass


class Prog:
    def __init__(self, sems):
        self.sem = sems
        self.q = {e: [] for e in ENGS}
        self.cnt = {k: 0 for k in SEMKEYS}
        self.waited = {e: {k: 0 for k in SEMKEYS} for e in ENGS}
        self.res = {}
        self.dnext = {q: 0 for q in NSLOT}
        self.ncc = 0

    def _deps(self, reads, writes):
        deps = {}

        def add(k, v):
            if deps.get(k, 0) < v:
                deps[k] = v

        for r in reads:
            st = self.res.get(r)
            if st and st["w"]:
                add(*st["w"])
        for w in writes:
            st = self.res.get(w)
            if st:
                if st["w"]:
                    add(*st["w"])
                for k, v in st["r"].items():
                    add(k, v)
        return deps

    def _commit(self, reads, writes, ev):
        for r in reads:
            st = self.res.setdefault(r, {"w": None, "r": {}})
            if st["r"].get(ev[0], 0) < ev[1]:
                st["r"][ev[0]] = ev[1]
        for w in writes:
            self.res[w] = {"w": ev, "r": {}}

    def _waits(self, eng, deps):
        for k, v in deps.items():
            if k == "pe":
                assert v <= self.cnt["pe"], "unresolved PE milestone"
            if self.waited[eng][k] < v:
                self.waited[eng][k] = v
                sem = self.sem[k]
                self.q[eng].append(lambda e, sem=sem, v=v: e.wait_ge(sem, v))

    def op(self, eng, fn, reads=(), writes=(), milestone=True):
        deps = self._deps(reads, writes)
        if eng == "pe":
            deps.pop("pe", None)
        self._waits(eng, deps)
        if eng == "pe" and not milestone:
            ev = ("pe", self.cnt["pe"] + 1)
            self.q[eng].append(fn)
        else:
            self.cnt[eng] += 1
            ev = (eng, self.cnt[eng])
            self.waited[eng][eng] = max(self.waited[eng][eng], 0)
            sem = self.sem[eng]
            self.q[eng].append(lambda e, fn=fn, sem=sem: fn(e).then_inc(sem, 1))
        self._commit(reads, writes, ev)

    def dma(self, qeng, fn, reads=(), writes=()):
        deps = self._deps(reads, writes)
        k = f"dq_{qeng}{self.dnext[qeng] % NSLOT[qeng]}"
        self.dnext[qeng] += 1
        if self.cnt[k] > 0 and deps.get(k, 0) < self.cnt[k]:
            deps[k] = self.cnt[k]
        self._waits(qeng, deps)
        self.cnt[k] += 16
        ev = (k, self.cnt[k])
        sem = self.sem[k]
        self.q[qeng].append(lambda e, fn=fn, sem=sem: fn(e).then_inc(sem, 16))
        self._commit(reads, writes, ev)

    def cc(self, fn, reads=(), writes=()):
        deps = self._deps(reads, writes)
        self._waits("pool", deps)
        k = f"cc{self.ncc}"
        self.ncc += 1
        self.cnt[k] += 1
        ev = (k, self.cnt[k])
        sem = self.sem[k]
        self.q["pool"].append(lambda e, fn=fn, sem=sem: fn(e).then_inc(sem, 1))
        self._commit(reads, writes, ev)

    def barrier(self, keep=()):
        kept = {n: self.res[n] for n in keep if n in self.res}
        skip = {st["w"][0] for st in kept.values() if st["w"] and st["w"][0].startswith("dq_")}
        tgt = {k: v for k, v in self.cnt.items() if k not in skip}
        for eng in ENGS:
            self._waits(eng, dict(tgt))
        self.res = kept


def build_program():
    nc = bass.Bass("TRN2", target_bir_lowering=False)

    def din(name, shape, dt=F32):
        return nc.dram_tensor(name, shape, dt, kind="ExternalInput")

    x_d = din("x", [T, D])
    xo_d = din("xo", [1024, D])
    cT_d = din("cT", [128, 16])
    wada_d = din("wada", [128, 16 * 3072])
    bada_d = din("bada", [1, 3072])
    nwT_d = din("nwT", [128, 16])
    win_d = din("win", [7, 128, 16 * 512])
    wglr_d = din("wglr", [128, 16 * 16])
    w2aug_d = din("w2aug", [17, 256])
    lam4_d = din("lam4", [4, 64])
    dnw_d = din("dnw", [1, 128])
    gnw_d = din("gnw", [1, 256])
    fnw_d = din("fnw", [1, D])
    wout_d = din("wout", [128, 16 * D])
    ident_d = din("ident", [128, 128])
    pm_d = din("pm", [128, 128])
    U_d = din("U", [128, 128])
    ind_d = din("ind", [128, 2])
    cos_d = din("cosT", [128, T])
    sin_d = din("sinT", [128, T])
    out_d = nc.dram_tensor("out", [1024, D], F32, kind="ExternalOutput")
    mss_in = nc.dram_tensor("mss_in", [1, 2048], F32)
    mss_out = nc.dram_tensor("mss_out", [2, 2048], F32)
    mg_in = nc.dram_tensor("mg_in", [1, 1024], F32)
    mg_out = nc.dram_tensor("mg_out", [2, 1024], F32)
    ya_in = nc.dram_tensor("ya_in", [T, 512], BF16)
    ya_out = nc.dram_tensor("ya_out", [2 * T, 512], BF16)
    yg_in = nc.dram_tensor("yg_in", [T, 512], BF16)
    yg_out = nc.dram_tensor("yg_out", [2 * T, 512], BF16)
    if DEBUG:
        dbg_hn = nc.dram_tensor("dbg_hn", [128, 16 * T], BF16, kind="ExternalOutput")
        dbg_y = nc.dram_tensor("dbg_y", [T, 1024], BF16, kind="ExternalOutput")
        dbg_q = nc.dram_tensor("dbg_q", [128, 2 * T], BF16, kind="ExternalOutput")
        dbg_mod = nc.dram_tensor("dbg_mod", [1, 3 * D], F32, kind="ExternalOutput")
        dbg_sm = nc.dram_tensor("dbg_sm", [128, 96], F32, kind="ExternalOutput")
        dbg_xh = nc.dram_tensor("dbg_xh", [128, D], BF16, kind="ExternalOutput")

    def bcast_rows(dh, n):
        return bass.AP(dh, 0, [[0, 128], [1, n]])

    with ExitStack() as top:
        sems = {k: top.enter_context(nc.semaphore("s_" + k)) for k in SEMKEYS}
        P = Prog(sems)
        PID = {}

        def sb(stack, name, shape, dt):
            return stack.enter_context(nc.sbuf_tensor("sb_" + name, shape, dt))

        ps = [top.enter_context(nc.psum_tensor(f"ps{i}", [128, 512], F32)) for i in range(7)]
        psT = top.enter_context(nc.psum_tensor("psT", [128, 8, 128], BF16))

        ident = sb(top, "ident", [128, 128], BF16)
        pm = sb(top, "pm", [128, 128], BF16)
        U = sb(top, "U", [128, 128], F32)
        ind = sb(top, "ind", [128, 2], F32)
        gateB = sb(top, "gateB", [128, D], F32)
        wA = sb(top, "wA", [128, 128], F32)
        wB = sb(top, "wB", [128, 256], F32)
        aT = sb(top, "aT", [128, 16], F32)
        shT = sb(top, "shT", [128, 16], F32)
        nwT = sb(top, "nwT", [128, 16], F32)
        neglam = sb(top, "neglam", [128, 1], F32)
        lamb = sb(top, "lamb", [128, 2, 2, 64], F32)
        lamp = sb(top, "lamp", [128, 2, 64], F32)
        lams = sb(top, "lams", [128, 2], F32)
        lame = sb(top, "lame", [128, 2], F32)
        one11 = sb(top, "one11", [1, 1], F32)
        onesrow = sb(top, "onesrow", [1, 128], F32)
        sm = sb(top, "sm", [128, 64], F32)

        P.dma("pool", lambda e: e.dma_start(out=ident[:], in_=ident_d.ap()), [], ["ident"])
        P.dma("pool", lambda e: e.dma_start(out=pm[:], in_=pm_d.ap()), [], ["pm"])
        P.dma("sp", lambda e: e.dma_start(out=U[:], in_=U_d.ap()), [], ["U"])
        P.dma("sp", lambda e: e.dma_start(out=ind[:], in_=ind_d.ap()), [], ["ind"])
        P.dma("sp", lambda e: e.dma_start(out=nwT[:], in_=nwT_d.ap()), [], ["nwT"])
        P.dma("sp", lambda e: e.dma_start(out=wA[:], in_=bcast_rows(dnw_d, 128)), [], ["wA"])
        P.dma("sp", lambda e: e.dma_start(out=wB[:], in_=bcast_rows(gnw_d, 256)), [], ["wB"])
        P.dma("sp", lambda e: e.dma_start(
            out=lamb[:].rearrange("p a b c -> p (a b c)"), in_=bcast_rows(lam4_d, 256)), [], ["lamb"])
        P.op("dve", lambda e: e.memset(one11[:], 1.0), [], ["one11"])
        P.op("dve", lambda e: e.memset(onesrow[:], 1.0), [], ["onesrow"])
        P.op("dve", lambda e: e.tensor_scalar(out=wA[:], in0=wA[:], scalar1=0.8, scalar2=None,
                                              op0=ALU.mult), ["wA"], ["wA"])
        P.op("dve", lambda e: e.tensor_tensor(
            out=lamp[:], in0=lamb[:, :, 0, :], in1=lamb[:, :, 1, :], op=ALU.mult), ["lamb"], ["lamp"])
        P.op("dve", lambda e: e.tensor_reduce(
            out=lams[:], in_=lamp[:], axis=mybir.AxisListType.X, op=ALU.add), ["lamp"], ["lams"])
        P.op("act", lambda e: e.activation(out=lame[:], in_=lams[:], func=AF.Exp), ["lams"], ["lame"])
        P.op("dve", lambda e: e.tensor_tensor(
            out=neglam[:], in0=lame[:, 1:2], in1=lame[:, 0:1], op=ALU.subtract), ["lame"], ["neglam"])
        P.op("dve", lambda e: e.tensor_scalar(out=neglam[:], in0=neglam[:], scalar1=-0.2, scalar2=None,
                                              op0=ALU.add), ["neglam"], ["neglam"])

        sH = None
        sW = None
        try:
            if STOP == -1:
                raise _Stop()
            sH = ExitStack()
            hnT = sb(sH, "hnT", [128, 16, T], BF16)
            pairs = [[0, 1], [2, 3], [4, 5], [6, 7]]
            sW = ExitStack()
            wbuf0_ = sb(sW, "wbuf0", [128, 16, 512], BF16)
            wglr = sb(sW, "wglr", [128, 16, 16], BF16)
            with ExitStack() as s01:
                cT_sb = sb(s01, "cT_sb", [128, 16], F32)
                cact = sb(s01, "cact", [128, 16], BF16)
                wring = [sb(s01, f"wring{i}", [128, 2048], BF16) for i in range(8)]
                badaB = sb(s01, "badaB", [1, 3072], F32)
                modsb = sb(s01, "modsb", [1, 3072], F32)
                modss = sb(s01, "modss", [1, 4096], F32)
                modg = modss
                xb = [sb(s01, f"xb{i}", [128, D], F32) for i in range(2)]
                junk = sb(s01, "junk", [128, D], BF16)
                xh = sb(s01, "xh", [128, 4, D], BF16)
                P.dma("sp", lambda e: e.dma_start(out=cT_sb[:], in_=cT_d.ap()), [], ["cT"])
                P.dma("sp", lambda e: e.dma_start(out=badaB[:], in_=bada_d.ap()), [], ["badaB"])
                def ss_dma(kk):
                    P.dma("pool", lambda e: e.dma_start(
                        out=wring[kk % 8][:], in_=wada_d.ap()[:, kk * 3072:kk * 3072 + 2048]),
                        [], [f"wring{kk % 8}"])

                def gate_dma(g_):
                    P.dma("pool", lambda e: e.dma_start(
                        out=wring[g_][:].rearrange("p (a c) -> p a c", a=2),
                        in_=wada_d.ap().rearrange("p (k c) -> p k c", k=16)[:, 2 * g_:2 * g_ + 2, 2048:3072]),
                        [], [f"wring{g_}"])

                for kk in range(8):
                    ss_dma(kk)
                P.op("act", lambda e: e.activation(out=cact[:], in_=cT_sb[:], func=AF.Silu), ["cT"], ["cact"])

                def front(t):
                    i = t % 4
                    b_ = xb[t % 2]
                    bn = f"xb{t % 2}"
                    c0 = 2 * (t % 8)
                    P.dma("sp", lambda e: e.dma_start(
                        out=b_[:], in_=x_d.ap()[t * 128:(t + 1) * 128, :]), [], [bn])
                    P.op("act", lambda e: e.activation(
                        out=junk[:], in_=b_[:], func=AF.Square, accum_out=sm[:, c0:c0 + 1]),
                        [bn], ["junk", f"sm{c0}"])
                    P.op("act", lambda e: e.activation(
                        out=sm[:, c0 + 1:c0 + 2], in_=sm[:, c0:c0 + 1], func=AF.Ln, scale=1.0 / D, bias=EPS),
                        [f"sm{c0}"], [f"sm{c0 + 1}"])
                    P.op("act", lambda e: e.activation(
                        out=sm[:, c0:c0 + 1], in_=sm[:, c0 + 1:c0 + 2], func=AF.Exp, scale=-0.5),
                        [f"sm{c0 + 1}"], [f"sm{c0}"])
                    P.op("dve", lambda e: e.tensor_scalar(
                        out=xh[:, i, :], in0=b_[:], scalar1=sm[:, c0:c0 + 1], scalar2=None, op0=ALU.mult),
                        [bn, f"sm{c0}"], [f"xh{i}"])

                for t in range(4):
                    front(t)

                for k in range(16):
                    for cg in range(4):
                        P.op("pe", lambda e, k=k, cg=cg: e.matmul(
                            ps[cg][0:1, 0:512], lhsT=cact[:, k:k + 1],
                            rhs=wring[k % 8][:, cg * 512:(cg + 1) * 512],
                            start=(k == 0), stop=(k == 15)), ["cact", f"wring{k % 8}"], [f"ps{cg}"],
                            milestone=(cg == 3))
                    if k + 8 < 16:
                        ss_dma(k + 8)
                for cg in range(4):
                    P.op("dve", lambda e, cg=cg: e.tensor_tensor(
                        out=modsb[0:1, cg * 512:(cg + 1) * 512], in0=ps[cg][0:1, 0:512],
                        in1=badaB[0:1, cg * 512:(cg + 1) * 512], op=ALU.add), [f"ps{cg}", "badaB"], ["modsb"])
                P.dma("sp", lambda e: e.dma_start(out=mss_in.ap(), in_=modsb[0:1, 0:2048]), ["modsb"], ["mss_in"])
                P.cc(lambda e: e.collective_compute(
                    "AllGather", ALU.bypass, replica_groups=pairs, ins=[mss_in.ap()], outs=[mss_out.ap()]),
                    ["mss_in"], ["mss_out"])
                P.dma("sp", lambda e: e.dma_start(
                    out=modss[0:1, :], in_=bass.AP(mss_out, 0, [[0, 1], [1, 4096]])), ["mss_out"], ["modss"])
                for j in range(32):
                    jj = j % 16
                    off = (jj // 8) * 2048 + (j // 16) * 1024 + (jj % 8) * 128
                    P.op("pe", lambda e, j=j, off=off: e.matmul(
                        ps[0][:, j:j + 1], lhsT=modss[0:1, off:off + 128], rhs=one11[0:1, 0:1],
                        start=True, stop=True), ["modss", "one11"], ["ps0"], milestone=(j == 31))
                P.op("dve", lambda e: e.scalar_tensor_tensor(
                    out=aT[:], in0=ps[0][:, 16:32], scalar=1.0, in1=nwT[:], op0=ALU.add, op1=ALU.mult),
                    ["ps0", "nwT"], ["aT"])
                P.op("dve", lambda e: e.tensor_copy(out=shT[:], in_=ps[0][:, 0:16]), ["ps0"], ["shT"])

                tb = [(psT[:, 0:4, :], "psT")]
                for bi_ in (4, 5, 6):
                    tb.append((ps[bi_][:, 0:256].bitcast(BF16).rearrange("p (a b) -> p a b", a=4), f"ps{bi_}"))
                def gate_front():
                    for k in range(16):
                        for cg in range(2):
                            P.op("pe", lambda e, k=k, cg=cg: e.matmul(
                                ps[cg][0:1, 0:512], lhsT=cact[:, k:k + 1],
                                rhs=wring[k // 2][:, (k % 2) * 1024 + cg * 512:(k % 2) * 1024 + (cg + 1) * 512],
                                start=(k == 0), stop=(k == 15)), ["cact", f"wring{k // 2}"], [f"ps{cg}"],
                                milestone=(cg == 1))
                    for cg in range(2):
                        P.op("dve", lambda e, cg=cg: e.tensor_tensor(
                            out=modsb[0:1, 2048 + cg * 512:2048 + (cg + 1) * 512], in0=ps[cg][0:1, 0:512],
                            in1=badaB[0:1, 2048 + cg * 512:2048 + (cg + 1) * 512], op=ALU.add),
                            [f"ps{cg}", "badaB"], ["modsb"])
                    P.dma("sp", lambda e: e.dma_start(out=mg_in.ap(), in_=modsb[0:1, 2048:3072]), ["modsb"], ["mg_in"])
                    P.cc(lambda e: e.collective_compute(
                        "AllGather", ALU.bypass, replica_groups=pairs, ins=[mg_in.ap()], outs=[mg_out.ap()]),
                        ["mg_in"], ["mg_out"])

                ti = 0
                for g_ in range(8):
                    s0_ = 2 * (g_ % 2)
                    for j in range(16):
                        tv, tn = tb[ti % 4]
                        ti += 1
                        for i in range(2):
                            P.op("pe", lambda e, i=i, j=j, tv=tv, s0_=s0_: e.transpose(
                                out=tv[:, i, :], in_=xh[:, s0_ + i, j * 128:(j + 1) * 128],
                                identity=ident[:]), [f"xh{s0_ + i}", "ident"], [tn], milestone=(i == 1))
                        srcv = tv[:, 0:2, :].rearrange("p a b -> p (a b)")
                        if j % 4 == 0:
                            P.op("act", lambda e, j=j, g_=g_, srcv=srcv: e.activation(
                                out=hnT[:, j, g_ * 256:(g_ + 1) * 256], in_=srcv, func=AF.Identity,
                                scale=aT[:, j:j + 1], bias=shT[:, j:j + 1]),
                                [tn, "aT", "shT"], [f"hnT_{g_}_{j}"])
                        else:
                            P.op("dve", lambda e, j=j, g_=g_, srcv=srcv: e.tensor_scalar(
                                out=hnT[:, j, g_ * 256:(g_ + 1) * 256], in0=srcv, scalar1=aT[:, j:j + 1],
                                scalar2=shT[:, j:j + 1], op0=ALU.mult, op1=ALU.add),
                                [tn, "aT", "shT"], [f"hnT_{g_}_{j}"])
                    if g_ + 2 < 8:
                        front(2 * (g_ + 2))
                        front(2 * (g_ + 2) + 1)
                    if 1 <= g_ <= 4:
                        gate_dma(2 * (g_ - 1))
                        gate_dma(2 * (g_ - 1) + 1)
                    if g_ == 5:
                        for kk in range(4):
                            P.dma("pool", lambda e, kk=kk: e.dma_start(
                                out=wbuf0_[:, kk * 4:(kk + 1) * 4, :].rearrange("p k c -> p (k c)"),
                                in_=win_d.ap()[4, :, kk * 2048:(kk + 1) * 2048]), [], [f"wbuf0_{kk}"])
                        P.dma("pool", lambda e: e.dma_start(
                            out=wglr[:].rearrange("p k c -> p (k c)"), in_=wglr_d.ap()), [], ["wglr"])
                    if g_ == 6:
                        gate_front()

                P.dma("sp", lambda e: e.dma_start(
                    out=modg[0:1, 0:2048], in_=bass.AP(mg_out, 0, [[0, 1], [1, 2048]])), ["mg_out"], ["modg", "modss"])
                for n in range(4):
                    P.op("pe", lambda e, n=n: e.matmul(
                        ps[2 + n % 2][:, 0:512], lhsT=onesrow[0:1, 0:128],
                        rhs=modg[0:1, n * 512:(n + 1) * 512], start=True, stop=True),
                        ["modg", "onesrow"], [f"ps{2 + n % 2}"])
                    P.op("act", lambda e, n=n: e.activation(
                        out=gateB[:, n * 512:(n + 1) * 512], in_=ps[2 + n % 2][:, 0:512], func=AF.Copy),
                        [f"ps{2 + n % 2}"], ["gateB"])
                P.barrier()
            if DEBUG:
                P.dma("sp", lambda e: e.dma_start(
                    out=dbg_hn.ap(), in_=hnT[:].rearrange("p a b -> p (a b)")), ["hnTall"], [])

            if STOP in (1, 11, 12):
                raise _Stop()
            wbuf = [wbuf0_, sb(sW, "wbuf1", [128, 16, 512], BF16)]
            ystg = [sb(sW, f"ystg{i}", [128, 256], BF16) for i in range(4)]
            ycnt = [0]

            SEQ = [4, 5, 6, 0, 1, 2, 3]
            WB = {g: i % 2 for i, g in enumerate(SEQ)}

            def after_group(g):
                i = SEQ.index(g)
                if i + 2 < len(SEQ):
                    load_group(SEQ[i + 2])

            def load_group(g):
                wb = wbuf[WB[g]]
                for kk in range(4):
                    P.dma("pool", lambda e, g=g, kk=kk, wb=wb: e.dma_start(
                        out=wb[:, kk * 4:(kk + 1) * 4, :].rearrange("p k c -> p (k c)"),
                        in_=win_d.ap()[g, :, kk * 2048:(kk + 1) * 2048]), [], [f"wbuf{WB[g]}_{kk}"])

            ipc = [0]
            ipbanks = [0, 1, 2]
            pend = []

            def inproj_block(g, blk, tg, M=128, w=None, wname=None):
                todo = pend[:]
                del pend[:]
                bi = ipbanks[ipc[0] % len(ipbanks)]
                ipc[0] += 1
                pst = ps[bi]
                for k in range(16):
                    if w is None:
                        lhs = lambda k=k: wbuf[WB[g]][:, k, blk * 128:blk * 128 + M]
                        wn = f"wbuf{WB[g]}_{k // 4}"
                    else:
                        lhs = lambda k=k: w[:, k, 0:M]
                        wn = wname
                    P.op("pe", lambda e, k=k, lhs=lhs, pst=pst: e.matmul(
                        pst[0:M, 0:512], lhsT=lhs(), rhs=hnT[:, k, tg * 512:(tg + 1) * 512],
                        start=(k == 0), stop=(k == 15)), [wn, "hnT"], [f"ps{bi}"], milestone=(k == 15))
                for f_ in todo:
                    f_()
                return pst, f"ps{bi}"

            def flush_pend():
                todo = pend[:]
                del pend[:]
                for f_ in todo:
                    f_()

            trc = [0]

            def evac_transposed(pst, pname, dst_fn, dname, tg, func, rawbufs, rprefix):
                ri = trc[0] % 2
                trc[0] += 1
                raw = rawbufs[ri]
                rn = f"{rprefix}{ri}"
                P.op("act", lambda e: e.activation(out=raw[:], in_=pst[:, 0:512], func=func), [pname], [rn])
                slot = ri

                def post():
                    for i in range(4):
                        P.op("pe", lambda e, i=i: e.transpose(
                            out=psT[:, slot * 4 + i, :], in_=raw[:, i * 128:(i + 1) * 128], identity=ident[:]),
                            [rn, "ident"], ["psT"], milestone=(i == 3))
                    P.op("dve", lambda e: e.tensor_copy(out=dst_fn(), in_=psT[:, slot * 4:(slot + 1) * 4, :]),
                         ["psT"], [dname])

                pend.append(post)

            def ywrite(src_fn, sname, t, c0, width):
                dst = ya_in if c0 < 512 else yg_in
                cc0 = c0 % 512
                P.dma("sp", lambda e: e.dma_start(
                    out=dst.ap()[t * 128:(t + 1) * 128, cc0:cc0 + width], in_=src_fn()), [sname], [f"y_in_{t}_{c0}"])

            load_group(5)

            with ExitStack() as sG:
                gqT = sb(sG, "gqT", [128, 2, T], BF16)
                gk = sb(sG, "gk", [128, NT, 256], BF16)
                gv = sb(sG, "gv", [128, NT, 512], BF16)
                gg = sb(sG, "gg", [128, NT, 512], BF16)
                glrT = sb(sG, "glrT", [32, T], F32)
                w2aug = sb(sG, "w2aug", [17, 256], F32)
                dec = sb(sG, "dec", [128, 2, 32], F32)
                raws = [sb(sG, f"graw{i}", [128, 512], BF16) for i in range(2)]
                e1 = [sb(sG, f"e1_{i}", [128, 256], F32) for i in range(2)]
                lap = [sb(sG, f"lap{i}", [128, 256], F32) for i in range(2)]
                ed = [sb(sG, f"ed{i}", [128, 256], F32) for i in range(2)]
                sqg = sb(sG, "sqg", [128, 256], F32)
                Sst = [sb(sG, f"Sst{i}", [128, 256], F32) for i in range(2)]
                Sbf = sb(sG, "Sbf", [128, 32, 256], BF16)
                gw = [sb(sG, f"gw{i}", [128, 256], F32) for i in range(2)]

                P.dma("sp", lambda e: e.dma_start(out=w2aug[:], in_=w2aug_d.ap()), [], ["w2aug"])
                P.op("pool", lambda e: e.memset(glrT[:], 1.0), [], ["glrT"])
                for blk in range(4):
                    e_ = blk // 2
                    for tg in range(4):
                        pst, pn = inproj_block(4, blk, tg)
                        if blk % 2 == 0:
                            P.op("act", lambda e, pst=pst, e_=e_, tg=tg: e.activation(
                                out=gqT[:, e_, tg * 512:(tg + 1) * 512], in_=pst[:, 0:512], func=AF.Copy,
                                scale=float(128 ** -0.5)), [pn], ["gqT"])
                        else:
                            evac_transposed(pst, pn, lambda e_=e_, tg=tg: gk[:, tg * 4:(tg + 1) * 4,
                                                                            e_ * 128:(e_ + 1) * 128],
                                            "gk", tg, AF.Copy, raws, "graw")
                after_group(4)
                for blk in range(4):
                    for tg in range(4):
                        pst, pn = inproj_block(5, blk, tg)
                        evac_transposed(pst, pn, lambda blk=blk, tg=tg: gv[:, tg * 4:(tg + 1) * 4,
                                                                           blk * 128:(blk + 1) * 128],
                                        "gv", tg, AF.Copy, raws, "graw")
                after_group(5)
                for blk in range(4):
                    for tg in range(4):
                        pst, pn = inproj_block(6, blk, tg)
                        evac_transposed(pst, pn, lambda blk=blk, tg=tg: gg[:, tg * 4:(tg + 1) * 4,
                                                                           blk * 128:(blk + 1) * 128],
                                        "gg", tg, AF.Silu, raws, "graw")
                after_group(6)
                for tg in range(4):
                    pst, pn = inproj_block(None, 0, tg, M=16, w=wglr, wname="wglr")
                    P.op("act", lambda e, pst=pst, tg=tg: e.activation(
                        out=glrT[0:16, tg * 512:(tg + 1) * 512], in_=pst[0:16, 0:512], func=AF.Copy),
                        [pn], ["glrT"])
                flush_pend()
                for t_ in range(NT):
                    for e_ in range(2):
                        P.op("dve", lambda e, t_=t_, e_=e_: e.tensor_tensor(
                            out=gg[:, t_, e_ * 256:(e_ + 1) * 256], in0=gg[:, t_, e_ * 256:(e_ + 1) * 256],
                            in1=wB[:], op=ALU.mult), ["gg", "wB"], ["gg"])
                def prep_a(t):
                    zb = 3 if t % 2 == 0 else 6
                    e1_, lap_ = e1[t % 2], lap[t % 2]
                    P.op("pe", lambda e: e.matmul(
                        ps[zb][:, 0:256], lhsT=glrT[0:17, t * 128:(t + 1) * 128], rhs=w2aug[0:17, :],
                        start=True, stop=True), ["glrT", "w2aug"], [f"ps{zb}"])
                    P.op("act", lambda e: e.activation(out=e1_[:], in_=ps[zb][:, 0:256], func=AF.Exp, scale=-1.0),
                         [f"ps{zb}"], [f"e1_{t % 2}"])
                    P.op("act", lambda e: e.activation(out=lap_[:], in_=e1_[:], func=AF.Ln, bias=1.0),
                         [f"e1_{t % 2}"], [f"lap{t % 2}"])

                def prep_b(t):
                    lap_, ed_ = lap[t % 2], ed[t % 2]
                    P.op("pe", lambda e: e.matmul(ps[4][:, 0:256], lhsT=U[:], rhs=lap_[:], start=True, stop=True),
                         ["U", f"lap{t % 2}"], ["ps4"])
                    for e_ in range(2):
                        P.op("pe", lambda e, e_=e_: e.matmul(
                            ps[5][:, e_ * 2:(e_ + 1) * 2], lhsT=lap_[:, e_ * 128:(e_ + 1) * 128], rhs=ind[:],
                            start=True, stop=True), [f"lap{t % 2}", "ind"], ["ps5"], milestone=(e_ == 1))
                    P.op("act", lambda e: e.activation(out=ed_[:], in_=ps[4][:, 0:256], func=AF.Exp,
                                                       scale=-1.0 / 16.0), ["ps4"], [f"ed{t % 2}"])
                    P.op("act", lambda e: e.activation(
                        out=dec[:, :, 2 * t:2 * t + 2],
                        in_=ps[5][:, 0:4].rearrange("p (a b) -> p a b", a=2), func=AF.Exp, scale=-1.0 / 16.0),
                        ["ps5"], ["dec"])
                    P.op("dve", lambda e: e.tensor_tensor(
                        out=gk[:, t, :], in0=gk[:, t, :], in1=ed_[:], op=ALU.mult), ["gk", f"ed{t % 2}"], ["gk"])

                prep_a(0)
                for t in range(NT):
                    if t + 1 < NT:
                        prep_a(t + 1)
                    prep_b(t)

                def scan_step(e_, c):
                    t, h = c // 2, c % 2
                    bi = 3 + c % 2
                    P.op("pe", lambda e: e.matmul(
                        ps[bi][:, 0:256], lhsT=gk[64 * h:64 * h + 64, t, e_ * 128:(e_ + 1) * 128],
                        rhs=gv[64 * h:64 * h + 64, t, e_ * 256:(e_ + 1) * 256], start=True, stop=True),
                        ["gk", "gv"], [f"ps{bi}"])
                    sn, so = Sst[(c + 1) % 2], Sst[c % 2]
                    if c == 0:
                        P.op("dve", lambda e: e.tensor_copy(out=sn[:], in_=ps[bi][:, 0:256]),
                             [f"ps{bi}"], [f"Sst{(c + 1) % 2}"])
                    else:
                        P.op("dve", lambda e: e.scalar_tensor_tensor(
                            out=sn[:], in0=so[:], scalar=dec[:, e_, c:c + 1], in1=ps[bi][:, 0:256],
                            op0=ALU.mult, op1=ALU.add),
                            [f"ps{bi}", f"Sst{c % 2}", "dec"], [f"Sst{(c + 1) % 2}"])
                    P.op("dve", lambda e: e.tensor_copy(out=Sbf[:, c, :], in_=sn[:]),
                         [f"Sst{(c + 1) % 2}"], [f"Sbf{c}"])

                def out_tile(e_, t):
                    bi = 5 + t % 2
                    for h in range(2):
                        c = 2 * t + h
                        P.op("pe", lambda e, c=c, h=h: e.matmul(
                            ps[bi][64 * h:64 * h + 64, 0:256], lhsT=gqT[:, e_, c * 64:(c + 1) * 64],
                            rhs=Sbf[:, c, :], start=True, stop=True, tile_position=(0, 64 * h)),
                            ["gqT", f"Sbf{c}"], [f"ps{bi}"], milestone=(h == 1))
                    c0 = 32 + 2 * (t % 8)
                    g_ = gw[t % 2]
                    ys = ystg[ycnt[0] % 4]
                    ysn = f"ystg{ycnt[0] % 4}"
                    ycnt[0] += 1
                    P.op("act", lambda e: e.activation(
                        out=sqg[:], in_=ps[bi][:, 0:256], func=AF.Square, accum_out=sm[:, c0:c0 + 1]),
                        [f"ps{bi}"], ["sqg", f"sm{c0}"])
                    P.op("act", lambda e: e.activation(
                        out=sm[:, c0 + 1:c0 + 2], in_=sm[:, c0:c0 + 1], func=AF.Ln, scale=1.0 / 256, bias=EPS),
                        [f"sm{c0}"], [f"sm{c0 + 1}"])
                    P.op("act", lambda e: e.activation(
                        out=sm[:, c0:c0 + 1], in_=sm[:, c0 + 1:c0 + 2], func=AF.Exp, scale=-0.5),
                        [f"sm{c0 + 1}"], [f"sm{c0}"])
                    P.op("dve", lambda e: e.scalar_tensor_tensor(
                        out=ys[:], in0=ps[bi][:, 0:256], scalar=sm[:, c0:c0 + 1],
                        in1=gg[:, t, e_ * 256:(e_ + 1) * 256], op0=ALU.mult, op1=ALU.mult),
                        [f"ps{bi}", f"sm{c0}", "gg"], [ysn])
                    ywrite(lambda: ys[:], ysn, t, 512 + e_ * 256, 256)

                for c in range(32):
                    scan_step(0, c)
                for t in range(NT):
                    out_tile(0, t)
                    scan_step(1, 2 * t)
                    scan_step(1, 2 * t + 1)
                for t in range(NT):
                    out_tile(1, t)
                P.barrier()

            pairs = [[0, 1], [2, 3], [4, 5], [6, 7]]
            P.cc(lambda e: e.collective_compute(
                "AllGather", ALU.bypass, replica_groups=pairs, ins=[yg_in.ap()], outs=[yg_out.ap()]),
                [], ["yg_out"])
            if STOP == 2:
                raise _Stop()
            with ExitStack() as sA:
                cosT = sb(sA, "cosT", [128, T], F32)
                sinT = sb(sA, "sinT", [128, T], F32)
                qT2 = [sb(sA, f"qT{i}", [128, T], BF16) for i in range(2)]
                kT2 = [[sb(sA, f"kT{i}_{c}", [128, T], BF16) for c in range(2)] for i in range(2)]
                Vp2 = [sb(sA, f"Vp{i}", [128, NT, 130], BF16) for i in range(2)]
                Gs2 = [sb(sA, f"Gs{i}", [128, NT, 128], BF16) for i in range(2)]
                raws = [sb(sA, f"araw{i}", [128, 512], BF16) for i in range(2)]
                rraw = [sb(sA, f"rraw{i}", [128, 512], BF16) for i in range(2)]
                t1 = [sb(sA, f"t1_{i}", [128, 512], F32) for i in range(2)]
                t2 = [sb(sA, f"t2_{i}", [128, 512], F32) for i in range(2)]
                NPT = 12
                pT = [sb(sA, f"pT{i}", [128, 2, 2, 128], BF16) for i in range(NPT)]
                o1 = [sb(sA, f"o1_{i}", [128, 128], F32) for i in range(6)]
                o2 = [sb(sA, f"o2_{i}", [128, 128], F32) for i in range(6)]
                gwa = [sb(sA, f"gwa{i}", [128, 128], F32) for i in range(6)]
                smE = sb(sA, "smE", [128, 48], F32)
                epc = [0]
                sqj = sb(sA, "sqj", [128, 128], F32)
                psTf = psT[:].rearrange("p a b -> p (a b)").bitcast(F32)
                P.dma("sp", lambda e: e.dma_start(out=cosT[:], in_=cos_d.ap()), [], ["cosT"])
                P.dma("sp", lambda e: e.dma_start(out=sinT[:], in_=sin_d.ap()), [], ["sinT"])
                for b_ in range(2):
                    P.op("pool", lambda e, b_=b_: e.memset(Vp2[b_][:], 1.0), [], [f"Vp{b_}"])
                    P.op("pool", lambda e, b_=b_: e.memset(kT2[b_][0][64:128, :], 0.0), [], [f"kT{b_}"])
                    P.op("pool", lambda e, b_=b_: e.memset(kT2[b_][1][0:64, :], 0.0), [], [f"kT{b_}"])
                rc = [0]
                pc = [0]
                sc = [0]
                del ipbanks[:]
                ipbanks.extend([0, 1])

                def inproj_closures(hd):
                    bb = hd % 2
                    out = []
                    for blk in range(4):
                        for tg in range(4):
                            def blockfn(blk=blk, tg=tg):
                                todo = pend[:]
                                del pend[:]
                                bi = ipbanks[ipc[0] % len(ipbanks)]
                                ipc[0] += 1
                                pst, pn = ps[bi], f"ps{bi}"
                                for k in range(16):
                                    P.op("pe", lambda e, k=k: e.matmul(
                                        pst[:, 0:512], lhsT=wbuf[WB[hd]][:, k, blk * 128:(blk + 1) * 128],
                                        rhs=hnT[:, k, tg * 512:(tg + 1) * 512], start=(k == 0), stop=(k == 15)),
                                        [f"wbuf{WB[hd]}_{k // 4}", "hnT"], [pn], milestone=(k == 15))
                                    if k % 4 == 3 and k < 15:
                                        yield
                                for f_ in todo:
                                    f_()
                                if blk < 2:
                                    ri = rc[0] % 2
                                    rc[0] += 1
                                    rr_, a1, a2 = rraw[ri], t1[ri], t2[ri]
                                    P.op("act", lambda e: e.activation(
                                        out=rr_[:], in_=pst[:, 0:512], func=AF.Copy), [pn], [f"rraw{ri}"])

                                    def rpost():
                                        P.op("pe", lambda e: e.matmul(
                                            psTf, lhsT=pm[:], rhs=rr_[:], start=True, stop=True),
                                            ["pm", f"rraw{ri}"], ["psT"])
                                        P.op("dve", lambda e: e.tensor_tensor(
                                            out=a1[:], in0=rr_[:], in1=cosT[:, tg * 512:(tg + 1) * 512],
                                            op=ALU.mult), [f"rraw{ri}", "cosT"], [f"t1_{ri}"])
                                        P.op("dve", lambda e: e.tensor_tensor(
                                            out=a2[:], in0=psTf, in1=sinT[:, tg * 512:(tg + 1) * 512],
                                            op=ALU.mult), ["psT", "sinT"], [f"t2_{ri}"])
                                        if blk == 0:
                                            P.op("dve", lambda e: e.tensor_tensor(
                                                out=qT2[bb][:, tg * 512:(tg + 1) * 512], in0=a1[:], in1=a2[:],
                                                op=ALU.add), [f"t1_{ri}", f"t2_{ri}"], [f"qT{bb}"])
                                        else:
                                            for c in range(2):
                                                P.op("dve", lambda e, c=c: e.tensor_tensor(
                                                    out=kT2[bb][c][64 * c:64 * c + 64, tg * 512:(tg + 1) * 512],
                                                    in0=a1[64 * c:64 * c + 64, :], in1=a2[64 * c:64 * c + 64, :],
                                                    op=ALU.add), [f"t1_{ri}", f"t2_{ri}"], [f"kT{bb}"])

                                    pend.append(rpost)
                                elif blk == 2:
                                    evac_transposed(pst, pn, lambda: Vp2[bb][:, tg * 4:(tg + 1) * 4, 0:128],
                                                    f"Vp{bb}", tg, AF.Copy, raws, "araw")
                                else:
                                    evac_transposed(pst, pn, lambda: Gs2[bb][:, tg * 4:(tg + 1) * 4, :],
                                                    f"Gs{bb}", tg, AF.Silu, raws, "araw")
                            out.append(blockfn)
                    return out

                for g_ in inproj_closures(0):
                    for _ in g_():
                        pass
                after_group(0)
                for hd in range(4):
                    bb = hd % 2
                    qT, kTc, Vp, Gs = qT2[bb], kT2[bb], Vp2[bb], Gs2[bb]
                    flush_pend()
                    fill = inproj_closures(hd + 1) if hd < 3 else []
                    steps = [(qi, [k_ for k_ in (kp, kp + 1) if k_ <= qi])
                             for qi in range(NT) for kp in range(0, qi + 1, 2)]
                    burst = {}

                    def acc_of(qi):
                        bank = 5 + qi % 2
                        return [ps[bank][:, 0:129], ps[bank][:, 256:385]], f"ps{bank}"

                    def emit_qk(qi, kis, qT=qT, kTc=kTc, bb=bb):
                        sbi = 2 + sc[0] % 3
                        sc[0] += 1
                        psS = ps[sbi][:, 0:512].rearrange("p (j c q) -> p j c q", j=2, c=2)
                        for j, ki in enumerate(kis):
                            for c in range(2):
                                P.op("pe", lambda e, j=j, ki=ki, c=c: e.matmul(
                                    psS[:, j, c, :], lhsT=kTc[c][:, ki * 128:(ki + 1) * 128],
                                    rhs=qT[:, qi * 128:(qi + 1) * 128], start=True, stop=True),
                                    [f"kT{bb}", f"qT{bb}"], [f"ps{sbi}"],
                                    milestone=(j == len(kis) - 1 and c == 1))
                        pi = pc[0] % NPT
                        pc[0] += 1
                        pt = pT[pi]
                        nk = len(kis)
                        P.op("act", lambda e: e.activation(
                            out=pt[:, 0:nk], in_=psS[:, 0:nk], func=AF.Exp, scale=0.125),
                            [f"ps{sbi}"], [f"pT{pi}"])
                        if qi in kis:
                            j = kis.index(qi)
                            P.op("dve", lambda e, j=j: e.memset(pt[64:128, j, :, 0:64], 0.0), [], [f"pT{pi}"])
                        return (qi, kis, pt, pi)

                    def emit_av(qi, kis, pt, pi, Vp=Vp, bb=bb):
                        accs, accn = acc_of(qi)
                        for j, ki in enumerate(kis):
                            P.op("pe", lambda e, j=j, ki=ki: e.matmul(
                                accs[0], lhsT=pt[:, j, 0, :], rhs=Vp[:, ki, 0:129],
                                start=(ki == 0), stop=(ki == qi)), [f"pT{pi}", f"Vp{bb}"], [accn],
                                milestone=(j == len(kis) - 1))
                            burst.setdefault(qi, []).append((pt, pi, j, ki))
                        if kis[-1] == qi:
                            items = burst.pop(qi)
                            for n_, (pt_, pi_, j_, ki_) in enumerate(items):
                                P.op("pe", lambda e, pt_=pt_, j_=j_, ki_=ki_, n_=n_: e.matmul(
                                    accs[1], lhsT=pt_[:, j_, 1, :], rhs=Vp[:, ki_, 0:129],
                                    start=(n_ == 0), stop=(n_ == len(items) - 1)), [f"pT{pi_}", f"Vp{bb}"],
                                    [accn], milestone=(n_ == len(items) - 1))
                            emit_epilogue(qi)

                    def emit_epilogue(qi, hd=hd, Gs=Gs, bb=bb):
                        accs, accn = acc_of(qi)
                        r6 = epc[0] % 6
                        epc[0] += 1
                        oa, ob, g_ = o1[r6], o2[r6], gwa[r6]
                        smn = f"smE{r6}"
                        b0 = 8 * r6
                        for c in range(2):
                            P.op("dve", lambda e, c=c: e.reciprocal(
                                out=smE[:, b0 + c:b0 + c + 1], in_=accs[c][:, 128:129]), [accn], [smn])
                        P.op("dve", lambda e: e.tensor_tensor(
                            out=smE[:, b0 + 2:b0 + 3], in0=smE[:, b0 + 1:b0 + 2], in1=neglam[:], op=ALU.mult),
                            [smn, "neglam"], [smn])
                        P.op("dve", lambda e: e.tensor_scalar(
                            out=oa[:], in0=accs[0][:, 0:128], scalar1=smE[:, b0:b0 + 1], scalar2=None,
                            op0=ALU.mult), [accn, smn], [f"o1_{r6}"])
                        P.op("dve", lambda e: e.scalar_tensor_tensor(
                            out=ob[:], in0=accs[1][:, 0:128], scalar=smE[:, b0 + 2:b0 + 3], in1=oa[:],
                            op0=ALU.mult, op1=ALU.add), [accn, smn, f"o1_{r6}"], [f"o2_{r6}"])
                        P.op("dve", lambda e: e.tensor_tensor(
                            out=g_[:], in0=Gs[:, qi, :], in1=wA[:], op=ALU.mult), [f"Gs{bb}", "wA"], [f"gwa{r6}"])
                        P.op("dve", lambda e: e.scalar_tensor_tensor(
                            out=sqj[:], in0=ob[:], scalar=1.0, in1=ob[:], op0=ALU.mult, op1=ALU.mult,
                            accum_out=smE[:, b0 + 3:b0 + 4]), [f"o2_{r6}"], ["sqj", smn + "s"])

                        def stage_b():
                            P.op("act", lambda e: e.activation(
                                out=smE[:, b0 + 4:b0 + 5], in_=smE[:, b0 + 3:b0 + 4], func=AF.Ln,
                                scale=1.0 / 128, bias=EPS), [smn + "s"], [smn + "t"])
                            P.op("act", lambda e: e.activation(
                                out=smE[:, b0 + 5:b0 + 6], in_=smE[:, b0 + 4:b0 + 5], func=AF.Exp, scale=-0.5),
                                [smn + "t"], [smn + "u"])
                            deferred.append([2, stage_c])

                        def stage_c():
                            ys = ystg[ycnt[0] % 4]
                            ysn = f"ystg{ycnt[0] % 4}"
                            ycnt[0] += 1
                            P.op("dve", lambda e: e.scalar_tensor_tensor(
                                out=ys[:, 0:128], in0=ob[:], scalar=smE[:, b0 + 5:b0 + 6], in1=g_[:],
                                op0=ALU.mult, op1=ALU.mult), [f"o2_{r6}", smn + "u", f"gwa{r6}"], [ysn])
                            ywrite(lambda: ys[:, 0:128], ysn, qi, hd * 128, 128)

                        deferred.append([4, stage_b])

                    def tick():
                        for d_ in deferred:
                            d_[0] -= 1
                        while deferred and deferred[0][0] <= 0:
                            deferred.pop(0)[1]()

                    curgen = [None]

                    def advance_fill():
                        if curgen[0] is None:
                            if not fill:
                                return
                            curgen[0] = fill.pop(0)()
                        try:
                            next(curgen[0])
                        except StopIteration:
                            curgen[0] = None

                    deferred = []
                    inflight = []
                    for si, (qi, kis) in enumerate(steps):
                        tick()
                        inflight.append(emit_qk(qi, kis))
                        if len(inflight) > 2:
                            emit_av(*inflight.pop(0))
                        for _ in range(FILL_BURST if si % FILL_EVERY == 1 else 0):
                            advance_fill()
                    while inflight:
                        tick()
                        emit_av(*inflight.pop(0))
                    while deferred:
                        deferred.pop(0)[1]()
                    while fill or curgen[0] is not None:
                        advance_fill()
                    if hd < 3:
                        after_group(hd + 1)
                    if hd == 2 and STOP > 37:
                        flush_pend()
                        for kk in range(8):
                            P.dma("pool", lambda e, kk=kk: e.dma_start(
                                out=hnT[:, kk * 2:(kk + 1) * 2, :].rearrange("p k c -> p (k c)"),
                                in_=wout_d.ap()[:, kk * 4096:(kk + 1) * 4096]),
                                [], (["hnT", "wout0"] if kk == 0 else [f"wout{kk}"]))
                P.barrier()
            sW.close()
            sW = None

            if STOP in (3, 31, 32, 33, 34, 35, 36, 37):
                raise _Stop()
            P.cc(lambda e: e.collective_compute(
                "AllGather", ALU.bypass, replica_groups=pairs, ins=[ya_in.ap()], outs=[ya_out.ap()]),
                [], ["ya_out"])
            if DEBUG:
                P.dma("sp", lambda e: e.dma_start(out=dbg_y.ap()[:, 0:512], in_=ya_in.ap()), [], [])
                P.dma("sp", lambda e: e.dma_start(out=dbg_y.ap()[:, 512:1024], in_=yg_in.ap()), [], [])

            if STOP == 4:
                raise _Stop()
            with ExitStack() as sO:
                wout = hnT
                fnwB = sb(sO, "fnwB", [128, D], F32)
                Yt = [sb(sO, f"Yt{i}", [128, 1024], BF16) for i in range(3)]
                yT = [sb(sO, f"yT{i}", [128, 8, 128], BF16) for i in range(2)]
                xo = [sb(sO, f"xo{i}", [128, D], F32) for i in range(3)]
                hb = [sb(sO, f"hb{i}", [128, D], F32) for i in range(8)]
                tmpm = [sb(sO, f"tmpm{i}", [128, 512], F32) for i in range(2)]
                junk2 = sb(sO, "junk2", [128, D], BF16)
                P.dma("sp", lambda e: e.dma_start(out=fnwB[:], in_=bcast_rows(fnw_d, D)), [], ["fnwB"])
                ycn = [0]

                def rank_of(e):
                    if "rank" not in PID:
                        PID["rank"] = e.partition_id() % 2
                    return PID["rank"]

                def loads(i, other):
                    slot = ycn[0] % 3
                    ycn[0] += 1
                    Y = Yt[slot]
                    for piece, (loc_t, gat_t, cname) in enumerate(((ya_in, ya_out, "ya_out"),
                                                                   (yg_in, yg_out, "yg_out"))):
                        def ld(e, i=i, Y=Y, piece=piece, loc_t=loc_t, gat_t=gat_t):
                            rank = rank_of(e)
                            if other:
                                src = gat_t.ap().rearrange("(q i p) c -> q i p c", q=4, i=8)
                                sel = src[ds(2 - rank, 1), i, :, :]
                            else:
                                src = loc_t.ap().rearrange("(h i p) c -> h i p c", h=2, i=8)
                                sel = src[ds(rank, 1), i, :, :]
                            return e.dma_start(out=Y[:, piece * 512:(piece + 1) * 512],
                                               in_=sel.rearrange("a p c -> (a p) c"))
                        P.dma("sp", ld, ([cname] if other else []), [f"Yt{slot}_{piece}"])
                    return slot

                def tchunk(slot, tslot, n):
                    Y, yt_ = Yt[slot], yT[tslot]
                    for a_ in range(4):
                        k = 4 * n + a_
                        P.op("pe", lambda e, a_=a_, k=k, Y=Y: e.transpose(
                            out=psT[:, a_, :], in_=Y[:, k * 128:(k + 1) * 128], identity=ident[:]),
                            [f"Yt{slot}_{k // 4}", "ident"], ["psT"], milestone=(a_ == 3))
                    P.op("act", lambda e, n=n, yt_=yt_: e.activation(
                        out=yt_[:, 4 * n:4 * n + 4, :], in_=psT[:, 0:4, :], func=AF.Copy),
                        ["psT"], [f"yT{tslot}"])

                def xload(i):
                    P.dma("sp", lambda e: e.dma_start(
                        out=xo[i % 3][:], in_=xo_d.ap()[i * 128:(i + 1) * 128, :]), [], [f"xo{i % 3}"])

                def scale_chunk(k):
                    P.op("dve", lambda e: e.tensor_tensor(
                        out=wout[:, k, :], in0=wout[:, k, :], in1=gateB[:], op=ALU.mult),
                        [f"wout{k // 2}", "gateB"], [f"wout{k // 2}"])

                for k in range(8):
                    scale_chunk(k)
                for phase in range(2):
                    other = phase == 1
                    if other:
                        for k in range(8, 16):
                            scale_chunk(k)
                    slots = {}
                    slots[0] = loads(0, other)
                    slots[1] = loads(1, other)
                    if not other:
                        xload(0)
                        xload(1)
                    for n in range(2):
                        tchunk(slots[0], 0, n)
                    for i in range(8):
                        yt_, hb_ = yT[i % 2], hb[i]
                        for n in range(4):
                            bi = n
                            for k in range(8):
                                kk = 8 * phase + k
                                P.op("pe", lambda e, k=k, kk=kk, n=n, bi=bi, yt_=yt_: e.matmul(
                                    ps[bi][:, 0:512], lhsT=yt_[:, k, :], rhs=wout[:, kk, n * 512:(n + 1) * 512],
                                    start=(k == 0), stop=(k == 7)), [f"yT{i % 2}", f"wout{kk // 2}"],
                                    [f"ps{bi}"], milestone=(k == 7))
                            if i + 1 < 8 and n < 2:
                                tchunk(slots[i + 1], (i + 1) % 2, n)
                            if not other:
                                xo_ = xo[i % 3]
                                P.op("dve", lambda e, bi=bi, n=n, xo_=xo_, hb_=hb_: e.tensor_tensor(
                                    out=hb_[:, n * 512:(n + 1) * 512], in0=ps[bi][:, 0:512],
                                    in1=xo_[:, n * 512:(n + 1) * 512], op=ALU.add),
                                    [f"ps{bi}", f"xo{i % 3}"], [f"hb{i}_{n}"])
                            else:
                                P.op("dve", lambda e, bi=bi, n=n, hb_=hb_: e.tensor_tensor(
                                    out=hb_[:, n * 512:(n + 1) * 512], in0=ps[bi][:, 0:512],
                                    in1=hb_[:, n * 512:(n + 1) * 512], op=ALU.add),
                                    [f"ps{bi}", f"hb{i}_{n}"], [f"hb{i}_{n}"])
                        if i + 2 < 8:
                            slots[i + 2] = loads(i + 2, other)
                            if not other:
                                xload(i + 2)
                        if other:
                            c0 = 2 * i
                            ob_ = xo[i % 3]
                            hn4 = [f"hb{i}_{n}" for n in range(4)]
                            P.op("act", lambda e, hb_=hb_, c0=c0: e.activation(
                                out=junk2[:], in_=hb_[:], func=AF.Square, accum_out=sm[:, c0:c0 + 1]),
                                hn4, ["junk2", f"sm{c0}"])
                            P.op("act", lambda e, c0=c0: e.activation(
                                out=sm[:, c0 + 1:c0 + 2], in_=sm[:, c0:c0 + 1], func=AF.Ln, scale=1.0 / D,
                                bias=EPS), [f"sm{c0}"], [f"sm{c0 + 1}"])
                            P.op("act", lambda e, c0=c0: e.activation(
                                out=sm[:, c0:c0 + 1], in_=sm[:, c0 + 1:c0 + 2], func=AF.Exp, scale=-0.5),
                                [f"sm{c0 + 1}"], [f"sm{c0}"])
                            P.op("dve", lambda e, hb_=hb_, ob_=ob_, c0=c0: e.scalar_tensor_tensor(
                                out=ob_[:], in0=hb_[:], scalar=sm[:, c0:c0 + 1], in1=fnwB[:], op0=ALU.mult,
                                op1=ALU.mult), hn4 + [f"sm{c0}", "fnwB"], [f"xo{i % 3}"])
                            P.dma("sp", lambda e, i=i, ob_=ob_: e.dma_start(
                                out=out_d.ap()[i * 128:(i + 1) * 128, :], in_=ob_[:]), [f"xo{i % 3}"],
                                ["out%d" % i])
                P.barrier()
            sH.close()
            sH = None
        except _Stop:
            P.barrier()
            for st_ in (sW, sH):
                if st_ is not None:
                    st_.close()

        with nc.Block() as block:
            @block.sync
            def _(e):
                for f in P.q["sp"]:
                    f(e)

            @block.scalar
            def _(e):
                for f in P.q["act"]:
                    f(e)

            @block.vector
            def _(e):
                for f in P.q["dve"]:
                    f(e)

            @block.gpsimd
            def _(e):
                for f in P.q["pool"]:
                    f(e)

            @block.tensor
            def _(e):
                for f in P.q["pe"]:
                    f(e)
    return nc


def _chunked(w):
    C = w.shape[1]
    return np.ascontiguousarray(w.reshape(16, 128, C).transpose(1, 0, 2).reshape(128, 16 * C))


def _consts():
    ident = np.eye(128, dtype=np.float32)
    pm = np.zeros((128, 128), np.float32)
    for m in range(128):
        partner = m + 32 if (m % 64) < 32 else m - 32
        pm[partner, m] = 1.0
    U = np.zeros((128, 128), np.float32)
    for j in range(128):
        for t in range(128):
            if j > t and j // 64 == t // 64:
                U[j, t] = 1.0
    ind = np.zeros((128, 2), np.float32)
    ind[:64, 0] = 1.0
    ind[64:, 1] = 1.0
    pos = np.arange(T, dtype=np.float32)
    inv_freq = (np.float32(10000.0) ** (-np.arange(0, 64, 2, dtype=np.float32) / np.float32(64))).astype(np.float32)
    ang = (pos[None, :] * inv_freq[:, None]).astype(np.float32)
    cos32 = np.cos(ang).astype(np.float32)
    sin32 = np.sin(ang).astype(np.float32)
    cosT = np.zeros((128, T), np.float32)
    sinT = np.zeros((128, T), np.float32)
    for p in range(128):
        cosT[p] = cos32[p % 32]
        sinT[p] = -sin32[p % 32] if (p % 64) < 32 else sin32[p % 32]
    return dict(ident=ident, pm=pm, U=U, ind=ind, cosT=cosT, sinT=sinT)


def _prepare_in_maps(x, c, norm_w, w_ada, b_ada, w_in, lambda_q1, lambda_k1, lambda_q2, lambda_k2,
                     diff_norm_w, gla_gate_w2, gla_gate_b, gla_norm_w, w_out, final_norm_w):
    f = lambda a: np.asarray(a, dtype=np.float32)
    x, c = f(x), f(c)
    w_ada, b_ada, w_in, w_out = f(w_ada)[0], f(b_ada)[0], f(w_in)[0], f(w_out)[0]
    consts = _consts()
    nwT = np.ascontiguousarray(f(norm_w)[0].reshape(16, 128).T)
    lam4 = np.stack([f(lambda_q1)[0], f(lambda_k1)[0], f(lambda_q2)[0], f(lambda_k2)[0]], 0)
    w2 = f(gla_gate_w2)[0]
    gb = f(gla_gate_b)[0]
    O = [0, 1024, 2048, 3072, 4096, 4608, 5120, 6144, 7168]
    in_maps = []
    for core in range(8):
        b, g = core // 2, core % 2
        groups = []
        for hd in range(4):
            H = 4 * g + hd
            cols = [w_in[:, O[i] + H * 128: O[i] + (H + 1) * 128] for i in range(4)]
            groups.append(np.concatenate(cols, 1))
        e0, e1 = 2 * g, 2 * g + 1
        groups.append(np.concatenate([w_in[:, O[4] + e0 * 128:O[4] + (e0 + 1) * 128],
                                      w_in[:, O[5] + e0 * 128:O[5] + (e0 + 1) * 128],
                                      w_in[:, O[4] + e1 * 128:O[4] + (e1 + 1) * 128],
                                      w_in[:, O[5] + e1 * 128:O[5] + (e1 + 1) * 128]], 1))
        groups.append(w_in[:, O[6] + e0 * 256:O[6] + (e1 + 1) * 256])
        groups.append(w_in[:, O[7] + e0 * 256:O[7] + (e1 + 1) * 256])
        win = np.stack([_chunked(gr) for gr in groups], 0)
        wglr = _chunked(w_in[:, O[8]:O[8] + 16])
        w2aug = np.concatenate([w2[:, e0 * 128:(e1 + 1) * 128], gb[None, e0 * 128:(e1 + 1) * 128]], 0)
        o_ = 1 - g
        perm = np.concatenate([np.arange(g * 512, (g + 1) * 512), np.arange(1024 + g * 512, 1024 + (g + 1) * 512),
                               np.arange(o_ * 512, (o_ + 1) * 512),
                               np.arange(1024 + o_ * 512, 1024 + (o_ + 1) * 512)])
        wout = _chunked(w_out[perm, :])
        m = dict(
            x=np.ascontiguousarray(x[b]),
            xo=np.ascontiguousarray(x[b, g * 1024:(g + 1) * 1024]),
            cT=np.ascontiguousarray(c[b].reshape(16, 128).T),
            wada=_chunked(np.concatenate([w_ada[:, j * 2048 + g * 1024:j * 2048 + (g + 1) * 1024]
                                          for j in range(3)], 1)),
            bada=np.ascontiguousarray(np.concatenate(
                [b_ada[j * 2048 + g * 1024:j * 2048 + (g + 1) * 1024] for j in range(3)])[None, :]),
            nwT=nwT, win=win, wglr=wglr, w2aug=np.ascontiguousarray(w2aug), lam4=np.ascontiguousarray(lam4),
            dnw=f(diff_norm_w)[0][None, :].copy(), gnw=f(gla_norm_w)[0][None, :].copy(),
            fnw=f(final_norm_w)[None, :].copy(), wout=wout,
        )
        m.update(consts)
        in_maps.append(m)
    return in_maps


_NC_CACHE = {}


def kernel(**inputs):
    in_maps = _prepare_in_maps(**inputs)
    if "nc" not in _NC_CACHE:
        _NC_CACHE["nc"] = build_program()
    nc = _NC_CACHE["nc"]
    res = run_bass_kernel_spmd(nc, in_maps, core_ids=list(range(8)))
    out = np.zeros((4, T, D), np.float32)
    for core in range(8):
        b, g = core // 2, core % 2
        out[b, g * 1024:(g + 1) * 1024] = np.asarray(res.results[core]["out"], dtype=np.float32)
    if DEBUG:
        kernel.last = res
    return out
```
